# Optimizing a Trainium2 kernel written in Bass

```python
import jax, jax.numpy as jnp
from jax import lax
import numpy as np

D_MODEL = 1024
BATCH = 8
SEQ = 2048
DEPTH = 2
DEC_BATCH = 128
DEC_SEQ = 1
PAST_LEN = 16384
PAGE_SIZE = 128

C_A = 3 * D_MODEL // 8
C_B = 3 * D_MODEL // 8
C_C = D_MODEL - C_A - C_B
N_HEADS_A = 4
N_HEADS_B = 4
N_HEADS_C = 4
HEAD_DIM_C = C_C // N_HEADS_C
D_MIX = C_A + C_B + C_C
K_A = 31
K_B = 3
CHUNK = 128
D_FF = -(-8 * D_MODEL // (3 * 256)) * 256
D_IN = 2 * C_A + 3 * C_B + 2 * C_C
EPS = 1e-6
SPLITS = (C_A, 2 * C_A, 2 * C_A + C_B, 2 * C_A + 2 * C_B, 2 * C_A + 3 * C_B, 2 * C_A + 3 * C_B + C_C)

kernel_name = "hybrid_conv_chunkmlp_decoder_step"


def rmsnorm(x, g):
    xf = x.astype(jnp.float32)
    r = lax.rsqrt(jnp.mean(xf * xf, axis=-1, keepdims=True) + EPS)
    return (xf * r).astype(x.dtype) * g


def layernorm(x, g, b):
    xf = x.astype(jnp.float32)
    mu = jnp.mean(xf, axis=-1, keepdims=True)
    var = jnp.mean(jnp.square(xf - mu), axis=-1, keepdims=True)
    return ((xf - mu) * lax.rsqrt(var + EPS)).astype(x.dtype) * g + b


def causal_dwconv(xh, w):
    return lax.conv_general_dilated(
        xh, w[:, None, :].astype(xh.dtype), window_strides=(1,), padding='VALID',
        dimension_numbers=('NWC', 'WIO', 'NWC'), feature_group_count=xh.shape[-1])


def chunk_spatial(v, w_s, b_s):
    n, L, _ = v.shape
    n_chunks = -(-L // CHUNK)
    vp = jnp.pad(v, ((0, 0), (0, n_chunks * CHUNK - L), (0, 0)))
    vp = vp.reshape(n, n_chunks, CHUNK, N_HEADS_C, HEAD_DIM_C)
    mask = jnp.tril(jnp.ones((CHUNK, CHUNK), dtype=w_s.dtype))
    out = jnp.einsum('hts,ncshd->ncthd', w_s * mask, vp) + jnp.transpose(b_s)[None, None, :, :, None]
    return out.reshape(n, n_chunks * CHUNK, C_C)[:, :L]


def mixer(h, buf_a, buf_b, w_in, dw_a, dw_a_bias, ln_a_g, ln_a_b, conv_b_w, ln_c_g, ln_c_b, w_s, b_s, w_o):
    z = h @ w_in
    a_val, a_gate, b_x, b_b, b_c, c_u, c_v = jnp.split(z, SPLITS, axis=-1)
    hist_a = jnp.concatenate([buf_a, a_val * jax.nn.sigmoid(a_gate)], axis=1)
    ya = jax.nn.silu(layernorm(causal_dwconv(hist_a, dw_a) + dw_a_bias, ln_a_g, ln_a_b))
    hist_b = jnp.concatenate([buf_b, b_c * b_x], axis=1)
    yb = b_b * causal_dwconv(hist_b, conv_b_w)
    u = jax.nn.gelu(c_u)
    vn = layernorm(jax.nn.gelu(c_v), ln_c_g, ln_c_b)
    yc = u * chunk_spatial(vn, w_s, b_s)
    out = jnp.concatenate([ya, yb, yc], axis=-1) @ w_o
    return out, hist_a[:, -(K_A - 1):], hist_b[:, -(K_B - 1):], vn


def trunk(x, bufs_a, bufs_b, norm_mix_g, w_in, dw_a, dw_a_bias, ln_a_g, ln_a_b, conv_b_w,
          ln_c_g, ln_c_b, w_s, b_s, w_o, norm_ffn_g, w_ffn_in, w_ffn_out, norm_final_g):
    new_a, new_b, new_v = [], [], []
    for i in range(DEPTH):
        m, ba, bb, vn = mixer(rmsnorm(x, norm_mix_g[i]), bufs_a[i], bufs_b[i], w_in[i], dw_a[i],
                              dw_a_bias[i], ln_a_g[i], ln_a_b[i], conv_b_w[i], ln_c_g[i], ln_c_b[i],
                              w_s[i], b_s[i], w_o[i])
        x = x + m
        gate, up = jnp.split(rmsnorm(x, norm_ffn_g[i]) @ w_ffn_in[i], 2, axis=-1)
        x = x + (jax.nn.silu(gate) * up) @ w_ffn_out[i]
        new_a.append(ba)
        new_b.append(bb)
        new_v.append(vn)
    return rmsnorm(x, norm_final_g), new_a, new_b, new_v


def setup_inputs(seed: int = 0) -> dict:
    key = jax.random.key(seed)
    ks = jax.random.split(key, 24)
    f32 = jnp.float32
    nrm = lambda k, shape, s: jax.random.normal(k, shape, f32) * s
    return {
        "x_prompt": nrm(ks[0], (BATCH, SEQ, D_MODEL), 1.0),
        "x_sample": nrm(ks[1], (DEC_BATCH, DEC_SEQ, D_MODEL), 1.0),
        "state_conv_a": nrm(ks[2], (DEPTH, DEC_BATCH, K_A - 1, C_A), 0.5),
        "state_conv_b": nrm(ks[3], (DEPTH, DEC_BATCH, K_B - 1, C_B), 0.5),
        "norm_mix_g": 1.0 + nrm(ks[4], (DEPTH, D_MODEL), 0.02),
        "w_in": nrm(ks[5], (DEPTH, D_MODEL, D_IN), D_MODEL ** -0.5),
        "dw_a": nrm(ks[6], (DEPTH, K_A, C_A), K_A ** -0.5),
        "dw_a_bias": nrm(ks[7], (DEPTH, C_A), 0.02),
        "ln_a_g": 1.0 + nrm(ks[8], (DEPTH, C_A), 0.02),
        "ln_a_b": nrm(ks[9], (DEPTH, C_A), 0.02),
        "conv_b_w": nrm(ks[10], (DEPTH, K_B, C_B), K_B ** -0.5),
        "ln_c_g": 1.0 + nrm(ks[11], (DEPTH, C_C), 0.02),
        "ln_c_b": nrm(ks[12], (DEPTH, C_C), 0.02),
        "w_s": nrm(ks[13], (DEPTH, N_HEADS_C, CHUNK, CHUNK), 0.5 * CHUNK ** -0.5),
        "b_s": 1.0 + nrm(ks[14], (DEPTH, N_HEADS_C, CHUNK), 0.02),
        "w_o": nrm(ks[15], (DEPTH, D_MIX, D_MODEL), D_MIX ** -0.5),
        "norm_ffn_g": 1.0 + nrm(ks[16], (DEPTH, D_MODEL), 0.02),
        "w_ffn_in": nrm(ks[17], (DEPTH, D_MODEL, 2 * D_FF), D_MODEL ** -0.5),
        "w_ffn_out": nrm(ks[18], (DEPTH, D_FF, D_MODEL), D_FF ** -0.5),
        "norm_final_g": 1.0 + nrm(ks[19], (D_MODEL,), 0.02),
    }


def reference(x_prompt, x_sample, state_conv_a, state_conv_b, norm_mix_g, w_in, dw_a, dw_a_bias,
              ln_a_g, ln_a_b, conv_b_w, ln_c_g, ln_c_b, w_s, b_s, w_o, norm_ffn_g, w_ffn_in,
              w_ffn_out, norm_final_g):
    weights = (norm_mix_g, w_in, dw_a, dw_a_bias, ln_a_g, ln_a_b, conv_b_w, ln_c_g, ln_c_b,
               w_s, b_s, w_o, norm_ffn_g, w_ffn_in, w_ffn_out, norm_final_g)
    zeros_a = jnp.zeros((DEPTH, x_prompt.shape[0], K_A - 1, C_A), x_prompt.dtype)
    zeros_b = jnp.zeros((DEPTH, x_prompt.shape[0], K_B - 1, C_B), x_prompt.dtype)
    y_prompt, pa, pb, _ = trunk(x_prompt, zeros_a, zeros_b, *weights)
    y_sample, sa, sb, sv = trunk(x_sample, state_conv_a, state_conv_b, *weights)
    new_conv_a_prompt = jnp.stack(pa, axis=0)
    new_conv_b_prompt = jnp.stack(pb, axis=0)
    new_conv_a_sample = jnp.stack(sa, axis=0)
    new_conv_b_sample = jnp.stack(sb, axis=0)
    new_chunk_v_sample = jnp.stack(sv, axis=0)
    return (y_prompt, y_sample, new_conv_a_prompt, new_conv_b_prompt, new_conv_a_sample, new_conv_b_sample, new_chunk_v_sample)
```

```python
from contextlib import ExitStack

import numpy as np
import concourse.bass as bass
import concourse.mybir as mybir
from concourse.bass_utils import run_bass_kernel_spmd

F32 = mybir.dt.float32
BF16 = mybir.dt.bfloat16
AF = mybir.ActivationFunctionType
ALU = mybir.AluOpType
AX = mybir.AxisListType

D = 1024
SEQ = 2048
NS = 16
CA = 384
CB = 384
CC = 256
KA = 31
KB = 3
DFF = 2816
NJ = DFF // 128
DIN = 2432
EPS = 1e-6
HALF = 1024
GELU_C = 1.5957691216057308


class Sem:
    def __init__(self, h, name):
        self.h = h
        self.name = name
        self.val = 0


class Buf:
    __slots__ = ("name", "w", "r")

    def __init__(self, name, pending=None):
        self.name = name
        self.w = dict(pending) if pending else {}
        self.r = {}


def _merge(dst, s, v):
    if dst.get(s, 0) < v:
        dst[s] = v


class K:
    def __init__(self, nc, stack, n_dma_sems=28):
        self.nc = nc
        self.eng = {"pe": nc.tensor, "act": nc.scalar, "dve": nc.vector, "pool": nc.gpsimd, "sp": nc.sync}
        self.esem = {e: Sem(stack.enter_context(nc.semaphore("c_" + e)), "c_" + e) for e in self.eng}
        self.seen = {e: {} for e in self.eng}
        self.dsems = [Sem(stack.enter_context(nc.semaphore("d%d" % i)), "d%d" % i) for i in range(n_dma_sems)]
        self.dnext = 0
        self.pending = {}
        self.nwaits = 0
        self.ninstr = 0

    def buf(self, name):
        return Buf(name, self.pending)

    def retire(self, bufs, reads_only=False):
        for b in bufs:
            if not reads_only:
                for s, v in b.w.items():
                    _merge(self.pending, s, v)
            for s, v in b.r.items():
                _merge(self.pending, s, v)

    def _deps(self, e, reads, writes):
        deps = {}
        for b in reads:
            for s, v in b.w.items():
                _merge(deps, s, v)
        for b in writes:
            for s, v in b.w.items():
                _merge(deps, s, v)
            for s, v in b.r.items():
                _merge(deps, s, v)
        if e == "pe":
            deps.pop(self.esem["pe"], None)
        return deps

    def _wait(self, e, deps):
        seen = self.seen[e]
        eng = self.eng[e]
        for s, v in deps.items():
            if seen.get(s, 0) >= v:
                continue
            eng.wait_ge(s.h, v)
            seen[s] = v
            self.nwaits += 1

    def _commit(self, tok, reads, writes):
        s, v = tok
        for b in writes:
            _merge(b.w, s, v)
        for b in reads:
            _merge(b.r, s, v)

    def op(self, e, fn, reads=(), writes=()):
        self._wait(e, self._deps(e, reads, writes))
        ins = fn(self.eng[e])
        s = self.esem[e]
        ins.then_inc(s.h, 1)
        s.val += 1
        self.ninstr += 1
        self._commit((s, s.val), reads, writes)

    def group(self, e, fns, reads=(), writes=()):
        self._wait(e, self._deps(e, reads, writes))
        ins = None
        for fn in fns:
            ins = fn(self.eng[e])
            self.ninstr += 1
        s = self.esem[e]
        ins.then_inc(s.h, 1)
        s.val += 1
        self._commit((s, s.val), reads, writes)

    def slot_sem(self, stack, name):
        return Sem(stack.enter_context(self.nc.semaphore(name)), name)

    def dma(self, e, out, in_, reads=(), writes=(), slot=None, **kw):
        self._wait(e, self._deps(e, reads, writes))
        if False and slot is not None:
            d = slot
            if d.val > 0:
                self.eng[e].sem_clear(d.h)
                for b in writes:
                    b.w.pop(d, None)
                for e2 in self.seen:
                    self.seen[e2].pop(d, None)
                d.val = 0
            self.eng[e].dma_start(out=out, in_=in_, **kw).then_inc(d.h, 16)
            d.val = 16
            self.ninstr += 1
            self._commit((d, 16), reads, writes)
            return
        d = self.dsems[self.dnext]
        self.dnext = (self.dnext + 1) % len(self.dsems)
        if d.val > 0:
            self._wait(e, {d: d.val})
        self.eng[e].dma_start(out=out, in_=in_, **kw).then_inc(d.h, 16)
        d.val += 16
        self.ninstr += 1
        self._commit((d, d.val), reads, writes)

    def finish(self):
        deps = {}
        for d in self.dsems:
            if d.val > 0:
                deps[d] = d.val
        for e, s in self.esem.items():
            if s.val > 0:
                deps[s] = s.val
        self._wait("sp", deps)


def build_program():
    nc = bass.Bass("TRN2", target_bir_lowering=False)

    def din(name, shape):
        return nc.dram_tensor(name, list(shape), F32, kind="ExternalInput").ap()

    def dout(name, shape):
        return nc.dram_tensor(name, list(shape), F32, kind="ExternalOutput").ap()

    xp = din("xp", [SEQ, D])
    xs = din("xs", [NS, D])
    sca = din("sca", [2, NS, KA - 1, CA])
    scb = din("scb", [2, NS, KB - 1, CB])
    norm_mix_g = din("norm_mix_g", [2, D])
    w_in = din("w_in", [2, D, DIN])
    dw_a = din("dw_a", [2, KA, CA])
    dw_a_bias = din("dw_a_bias", [2, CA])
    ln_a_g = din("ln_a_g", [2, CA])
    ln_a_b = din("ln_a_b", [2, CA])
    conv_b_w = din("conv_b_w", [2, KB, CB])
    ln_c_g = din("ln_c_g", [2, CC])
    ln_c_b = din("ln_c_b", [2, CC])
    w_s = din("w_s", [2, 4, 128, 128])
    b_s = din("b_s", [2, 4, 128])
    w_o = din("w_o", [2, D, D])
    norm_ffn_g = din("norm_ffn_g", [2, D])
    w_ffn_in = din("w_ffn_in", [2, D, 2 * DFF])
    w_ffn_out = din("w_ffn_out", [2, DFF, D])
    norm_final_g = din("norm_final_g", [D])

    yp = dout("yp", [SEQ, D])
    ys = dout("ys", [NS, D])
    ncap = dout("ncap", [2, KA - 1, CA])
    ncbp = dout("ncbp", [2, KB - 1, CB])
    ncas = dout("ncas", [2, NS, KA - 1, CA])
    ncbs = dout("ncbs", [2, NS, KB - 1, CB])
    ncv = dout("ncv", [2, NS, CC])

    with ExitStack() as st:
        k = K(nc, st)

        uid = [0]

        def sb(stack, name, shape, dt):
            uid[0] += 1
            return stack.enter_context(nc.sbuf_tensor("%s_%d" % (name, uid[0]), list(shape), dt))

        dscr = nc.dram_tensor("dscr", [2, 128, 3 * (KA + KB) * 128], BF16, kind="Internal").ap()
        b_dscr = [k.buf("dscr0"), k.buf("dscr1")]

        NT = HALF + NS
        xT = sb(st, "xT", [128, 8, NT], F32)
        hy = sb(st, "hy", [128, 8, NT], BF16)
        b_x = [k.buf("x%d" % i) for i in range(3)]
        b_hy = [k.buf("hy%d" % i) for i in range(5)]

        def hyb(bi):
            return [b_hy[2 * bi], b_hy[2 * bi + 1]] if bi < 2 else [b_hy[4]]

        NRIN = 4
        rfi0 = sb(st, "rfi0p", [128, 8, 2, 128], BF16)
        b_rfi0 = k.buf("rfi0p")
        b_wbb = [k.buf("wbb%d" % i) for i in range(3)]
        ident_f = sb(st, "ident_f", [128, 128], F32)
        ident_b = sb(st, "ident_b", [128, 128], BF16)
        ones_b = sb(st, "ones_b", [128, 128], BF16)
        epsc = sb(st, "epsc", [128, 1], F32)
        b_const = k.buf("const")
        pAB = sb(st, "pAB", [128, 2, 3, 37], F32)
        pC = sb(st, "pC", [128, 2, 2, 2], F32)
        pD = sb(st, "pD", [128, 8, 5], F32)
        WmT = sb(st, "WmT", [128, 2, 4, 128], BF16)
        bsb = sb(st, "bsb", [128, 2, 2, 128], F32)
        w00 = sb(st, "w00", [128, 2, 2], F32)
        b_par = k.buf("par")
        b_pAB, b_pC, b_WmT, b_bsb = k.buf("pAB"), k.buf("pC"), k.buf("WmT"), k.buf("bsb")
        bias2 = sb(st, "bias2", [128, 2, 2, 128], F32)
        b_bias2 = k.buf("bias2")
        negh = sb(st, "negh", [128, 2], F32)
        b_pD = k.buf("pD")
        carry_a = sb(st, "carry_a", [128, 2, 3, KA - 1], BF16)
        carry_b = sb(st, "carry_b", [128, 2, 3, KB - 1], BF16)
        b_carry = k.buf("carry")
        sqr = [sb(st, "sqr%d" % i, [128, 512], BF16) for i in range(3)]
        b_sqr = [k.buf("sqr%d" % i) for i in range(3)]
        nsd = sb(st, "nsd", [128, 512], F32)
        nrs = sb(st, "nrs", [128, 512], F32)
        b_nsd, b_nrs = k.buf("nsd"), k.buf("nrs")

        ps = [st.enter_context(nc.psum_tensor("ps%d" % i, [128, 512], F32)) for i in range(8)]
        b_ps = [k.buf("ps%d" % i) for i in range(8)]
        psn = [0]

        def nextps():
            i = psn[0]
            psn[0] = (i + 1) % 8
            return i

        cnt = {"rin": 0, "ro": 0, "rfi": 0, "rfo": 0, "sqr": 0, "ev": 0}

        def rr(name, n):
            i = cnt[name]
            cnt[name] = (i + 1) % n
            return i

        def act(func, out, in_, reads, writes, scale=1.0, bias=0.0):
            k.op("act", lambda e: e.activation(out=out, in_=in_, func=func, bias=bias, scale=scale), reads, writes)

        def acopy(out, in_, reads, writes):
            k.op("act", lambda e: e.copy(out=out, in_=in_), reads, writes)

        def vcopy(out, in_, reads, writes):
            k.op("dve", lambda e: e.tensor_copy(out=out, in_=in_), reads, writes)

        def evac(out, in_, reads, writes):
            cnt["ev"] += 1
            if cnt["ev"] % 3 == 0:
                vcopy(out, in_, reads, writes)
            else:
                acopy(out, in_, reads, writes)

        def tt(out, a, b, op, reads, writes):
            k.op("dve", lambda e: e.tensor_tensor(out=out, in0=a, in1=b, op=op), reads, writes)

        def stt(out, in0, scalar, in1, op0, op1, reads, writes):
            k.op("dve", lambda e: e.scalar_tensor_tensor(out=out, in0=in0, scalar=scalar, in1=in1, op0=op0, op1=op1),
                 reads, writes)

        def tsc1(out, in0, s1, op0, reads, writes):
            k.op("dve", lambda e: e.tensor_scalar(out=out, in0=in0, scalar1=s1, scalar2=None, op0=op0), reads, writes)

        def tsc(out, in0, s1, s2, op0, op1, reads, writes):
            k.op("dve", lambda e: e.tensor_scalar(out=out, in0=in0, scalar1=s1, scalar2=s2, op0=op0, op1=op1),
                 reads, writes)

        def mm_group(pi, n, pairs, reads):
            np_ = len(pairs)
            fns = []
            for i, (l, r) in enumerate(pairs):
                fns.append(lambda e, l=l, r=r, i=i: e.matmul(ps[pi][:, 0:n], lhsT=l, rhs=r,
                                                           start=(i == 0), stop=(i == np_ - 1)))
            k.group("pe", fns, reads=reads, writes=[b_ps[pi]])

        def transposes(items, reads, writes):
            fns = [(lambda e, o=o, i=i, d=d: e.transpose(out=o, in_=i, identity=d)) for (o, i, d) in items]
            k.group("pe", fns, reads=reads, writes=writes)

        k.op("pool", lambda e: e.memset(ident_f[:], 0.0), writes=[b_const])
        k.op("pool", lambda e: e.affine_select(out=ident_f[:], in_=ident_f[:], pattern=[[-1, 128]],
                                               compare_op=ALU.not_equal, fill=1.0, base=0, channel_multiplier=1),
             reads=[b_const], writes=[b_const])
        k.op("dve", lambda e: e.memset(ones_b[:], 1.0), writes=[b_const])
        k.op("dve", lambda e: e.memset(epsc[:], EPS), writes=[b_const])
        k.op("dve", lambda e: e.memset(negh[:], -0.5), writes=[b_const])
        acopy(ident_b[:], ident_f[:], [b_const], [b_const])

        s0 = ExitStack()
        sD = ExitStack()
        s1h0 = ExitStack()
        pjobs = {}

        def p_alloc(scope, tag, C):
            pjobs[tag] = [sb(scope, "stg" + tag, [40, C], F32), k.buf("stg" + tag), 0, C]

        def p_issue(tag, rows):
            stg, b_stg, R, C = pjobs[tag]
            for ap, r in rows:
                k.dma("sp", stg[R:R + r, 0:C], ap, writes=[b_stg])
                R += r
            pjobs[tag][2] = R

        def p_fin(tag, dst_fn, b_dst=None):
            b_dst = b_dst or b_par
            stg, b_stg, R, C = pjobs[tag]
            for c in range(C // 128):
                pi = nextps()
                transposes([(ps[pi][:, 0:R], stg[0:R, c * 128:(c + 1) * 128], ident_f[0:R, 0:R])],
                           [b_stg, b_const], [b_ps[pi]])
                acopy(dst_fn(c), ps[pi][:, 0:R], [b_ps[pi]], [b_dst])
            k.retire([b_stg])

        for l in range(2):
            p_alloc(s0, "AB%d" % l, CA)
            p_alloc(s0, "C%d" % l, CC)
        wsts = {}
        for l in range(2):
            for h in range(4):
                wsts[(l, h)] = (sb(s0, "wst%d%d" % (l, h), [128, 128], F32), k.buf("wst%d%d" % (l, h)))
        p_alloc(sD, "D", D)
        xin0 = [sb(s1h0, "xin%d" % i, [128, 1024], F32) for i in range(8)]
        b_xin0 = [k.buf("xin%d" % i) for i in range(8)]
        p_issue("D", [(norm_mix_g, 2), (norm_ffn_g, 2), (norm_final_g.unsqueeze(0), 1)])
        for ti in range(8):
            k.dma("sp", xin0[ti][:], xp[ti * 128:(ti + 1) * 128, :], writes=[b_xin0[ti]])
        for l in range(2):
            p_issue("AB%d" % l, [(dw_a[l], KA), (dw_a_bias[l:l + 1, :], 1), (ln_a_g[l:l + 1, :], 1),
                                 (ln_a_b[l:l + 1, :], 1), (conv_b_w[l], KB)])
            p_issue("C%d" % l, [(ln_c_g[l:l + 1, :], 1), (ln_c_b[l:l + 1, :], 1)])
            for h in range(4):
                wst, b_wst = wsts[(l, h)]
                k.dma("sp", wst[:], w_s[l, h], writes=[b_wst])
                q, hh = h // 2, h % 2
                k.dma("sp", bsb[hh * 64:(hh + 1) * 64, l, q, :], b_s[l, h].partition_broadcast(64), writes=[b_bsb])
                k.dma("sp", w00[hh * 64:(hh + 1) * 64, l, q:q + 1],
                      w_s[l, h, 0, 0:1].partition_broadcast(64), writes=[b_bsb])
        def finish_vec(l):
            p_fin("AB%d" % l, lambda c, l=l: pAB[:, l, c, :], b_pAB)
            p_fin("C%d" % l, lambda c, l=l: pC[:, l, c, :], b_pC)

        def finish_wm(l):
            for h in range(4):
                wst, b_wst = wsts[(l, h)]
                k.op("pool", lambda e, wst=wst: e.affine_select(out=wst[:], in_=wst[:], pattern=[[-1, 128]],
                                                                compare_op=ALU.is_ge, fill=0.0, base=0,
                                                                channel_multiplier=1),
                     reads=[b_wst], writes=[b_wst])
                pi = nextps()
                transposes([(ps[pi][:, 0:128], wst[:], ident_f[:])], [b_wst, b_const], [b_ps[pi]])
                acopy(WmT[:, l, h, :], ps[pi][:, 0:128], [b_ps[pi]], [b_WmT])
                k.retire([b_wst])
            for q in range(2):
                pi = nextps()
                fns = [(lambda e, hh=hh, pi=pi: e.matmul(ps[pi][hh * 64:(hh + 1) * 64, 0:128], lhsT=ones_b[:, 0:64],
                                                        rhs=WmT[:, l, 2 * q + hh, :], start=True, stop=True))
                       for hh in range(2)]
                k.group("pe", fns, reads=[b_WmT, b_const], writes=[b_ps[pi]])
                stt(bias2[:, l, q, :], ps[pi][:, 0:128], pC[:, l, q, 1:2], bsb[:, l, q, :], ALU.mult, ALU.add,
                    [b_ps[pi], b_pC, b_bsb], [b_bias2])

        def rmsnorm(col0, n, b_src, gcol, dst_fn, b_dst):
            cols = slice(col0, col0 + n)
            pi = nextps()
            for kk in range(8):
                si = rr("sqr", 3)
                act(AF.Square, sqr[si][:, 0:n], xT[:, kk, cols], [b_src], [b_sqr[si]])
                k.op("pe", lambda e, kk=kk, si=si: e.matmul(ps[pi][:, 0:n], lhsT=ones_b[:], rhs=sqr[si][:, 0:n],
                                                           start=(kk == 0), stop=(kk == 7)),
                     reads=[b_sqr[si], b_const], writes=[b_ps[pi]])
            act(AF.Ln, nsd[:, 0:n], ps[pi][:, 0:n], [b_ps[pi], b_const], [b_nsd], scale=1.0 / D, bias=epsc[:, 0:1])
            act(AF.Exp, nrs[:, 0:n], nsd[:, 0:n], [b_nsd], [b_nrs], scale=-0.5)
            for kk in range(8):
                stt(dst_fn(kk), xT[:, kk, cols], pD[:, kk, gcol:gcol + 1], nrs[:, 0:n], ALU.mult, ALU.mult,
                    [b_src, b_nrs, b_pD], b_dst)

        for half in range(2):
            wblocks = [(0, 512, 0), (512, 512, 1)] + ([(HALF, NS, 2)] if half == 1 else [])
            sblocks = [(i * 256, 256, i, False) for i in range(4)] + ([(HALF, NS, 4, True)] if half == 1 else [])

            with ExitStack() as s1:
                if half == 0:
                    xin, b_xin = xin0, b_xin0
                else:
                    xin, b_xin, xsn, b_xsn = xin1, b_xin1, xsn1, b_xsn1
                for ti in range(8):
                    for hh in range(2):
                        pi = nextps()
                        transposes([(ps[pi][:, c * 128:(c + 1) * 128], xin[ti][:, (4 * hh + c) * 128:(4 * hh + c + 1) * 128],
                                     ident_f[:]) for c in range(4)], [b_xin[ti], b_const], [b_ps[pi]])
                        evac(xT[:, 4 * hh:4 * hh + 4, ti * 128:(ti + 1) * 128],
                             ps[pi][:].rearrange("p (c t) -> p c t", c=4), [b_ps[pi]], [b_x[ti // 4]])
                if half == 1:
                    pi = nextps()
                    transposes([(ps[pi][:, c * NS:(c + 1) * NS], xsn[0:NS, c * 128:(c + 1) * 128],
                                 ident_f[0:NS, 0:NS]) for c in range(8)], [b_xsn, b_const], [b_ps[pi]])
                    acopy(xT[:, :, HALF:HALF + NS], ps[pi][:, 0:8 * NS].rearrange("p (c t) -> p c t", c=8),
                         [b_ps[pi]], [b_x[2]])
                    k.retire([b_xsn])
                k.retire(b_xin)
                if half == 0:
                    p_fin("D", lambda c: pD[:, c, :], b_pD)
                    s1h0.close()
                    sD.close()
                else:
                    s1h1.close()

            for l in range(2):
                with ExitStack() as sm:
                    NTAP = KA + KB
                    diag = sb(sm, "diag", [128, 3, NTAP, 128], BF16)
                    ring_in = [sb(sm, "rin%d" % i, [128, 8, 256], BF16) for i in range(NRIN)]
                    b_rin = [k.buf("rin%d" % i) for i in range(NRIN)]
                    ring_o = [sb(sm, "ro%d" % i, [128, 8, 256], BF16) for i in range(4)]
                    b_ro = [k.buf("ro%d" % i) for i in range(4)]
                    hist_a = sb(sm, "hist_a", [128, 3, KA - 1 + HALF], BF16)
                    hist_b = sb(sm, "hist_b", [128, 3, KB - 1 + HALF], BF16)
                    wbb = sb(sm, "wbb", [128, 8, 384], BF16)
                    u_t = sb(sm, "u_t", [128, 2, NT], BF16)
                    gv = sb(sm, "gv", [128, 2, NT], F32)
                    tail_a = sb(sm, "tail_a", [128, 3, KA - 1], F32)
                    tail_b = sb(sm, "tail_b", [128, 3, KB - 1], F32)
                    osb = sb(sm, "osb", [32, 384], F32)
                    names = ["diag0", "diag1", "diag2", "hist_a", "hist_b", "u", "gv", "tail", "osb"]
                    B = {n_: k.buf(n_) for n_ in names}
                    if half == 1:
                        has = sb(sm, "has", [128, 3, NS, KA], F32)
                        hbs = sb(sm, "hbs", [128, 3, NS, KB], F32)
                        stg2 = [sb(sm, "stg2_%d" % i, [120, 384], F32) for i in range(4)]
                        stgb = sb(sm, "stgb", [32, 384], F32)
                        cvs = sb(sm, "cvs", [128, 3, NS], F32)
                        cvb = sb(sm, "cvb", [128, 3, NS], F32)
                        prod = sb(sm, "prod", [128, NS, KA], F32)
                        for n_ in ["has", "hbs", "stg20", "stg21", "stg22", "stg23", "stgb", "cvs", "cvb", "prod"]:
                            B[n_] = k.buf(n_)
                    mix_bufs = list(B.values())
                    for b_ in b_wbb:
                        for s_, v_ in k.pending.items():
                            _merge(b_.w, s_, v_)

                    cnt2 = {"sig": 0, "bcs": 0, "g": 0, "t": 0, "cbs": 0, "dg": 0}

                    def r2(name):
                        i = cnt2[name]
                        cnt2[name] = (i + 1) % 2
                        return i

                    if half == 0:
                        k.op("dve", lambda e: e.memset(hist_a[:, :, 0:KA - 1], 0.0), writes=[B["hist_a"]])
                        k.op("dve", lambda e: e.memset(hist_b[:, :, 0:KB - 1], 0.0), writes=[B["hist_b"]])
                    else:
                        vcopy(hist_a[:, :, 0:KA - 1], carry_a[:, l], [b_carry], [B["hist_a"]])
                        vcopy(hist_b[:, :, 0:KB - 1], carry_b[:, l], [b_carry], [B["hist_b"]])
                        k.dma("sp", diag[:].rearrange("p j t m -> p (j t m)"), dscr[l], reads=[b_dscr[l]], writes=[B["diag0"], B["diag1"], B["diag2"]])

                    if half == 1:
                        for i in range(4):
                            k.dma("pool", stg2[i][:], sca[l, 4 * i:4 * i + 4].rearrange("n k c -> (n k) c"),
                                  writes=[B["stg2%d" % i]])
                        k.dma("pool", stgb[:], scb[l].rearrange("n k c -> (n k) c"), writes=[B["stgb"]])
                    for (c0, n, bi) in wblocks:
                        rmsnorm(c0, n, b_x[bi], l, lambda kk, c0=c0, n=n: hy[:, kk, c0:c0 + n], hyb(bi))
                    if half == 0 and l == 0:
                        finish_vec(0)

                    if half == 1:
                        for i in range(4):
                            si = i
                            for j in range(3):
                                pi = nextps()
                                transposes([(ps[pi][:, 0:120], stg2[si][0:120, j * 128:(j + 1) * 128],
                                             ident_f[0:120, 0:120])], [B["stg2%d" % si], b_const], [b_ps[pi]])
                                acopy(has[:, j, 4 * i:4 * i + 4, 0:KA - 1],
                                     ps[pi][:, 0:120].rearrange("p (n k) -> p n k", n=4), [b_ps[pi]], [B["has"]])
                        for j in range(3):
                            pi = nextps()
                            transposes([(ps[pi][:, 0:32], stgb[0:32, j * 128:(j + 1) * 128], ident_f[0:32, 0:32])],
                                       [B["stgb"], b_const], [b_ps[pi]])
                            acopy(hbs[:, j, :, 0:KB - 1], ps[pi][:, 0:32].rearrange("p (n k) -> p n k", n=NS),
                                 [b_ps[pi]], [B["hbs"]])
                        k.dma("sp", ncas[l, :, 0:KA - 2, :], sca[l, :, 1:KA - 1, :])
                        k.dma("sp", ncbs[l, :, 0:KB - 2, :], scb[l, :, 1:KB - 1, :])

                    diag_jobs = []
                    if half == 0:
                        for j in range(3):
                            diag_jobs.append((j, 0, KA, 0))
                            diag_jobs.append((j, KA, KB, 34))

                    def build_diag(njobs):
                        for _ in range(njobs):
                            if not diag_jobs:
                                return
                            j, t0, nt, col = diag_jobs.pop(0)
                            tt(diag[:, j, t0:t0 + nt, :], ident_b[:].unsqueeze(1).broadcast_to([128, nt, 128]),
                               pAB[:, l, j, col:col + nt].unsqueeze(2).broadcast_to([128, nt, 128]), ALU.mult,
                               [b_const, b_pAB], [B["diag%d" % j]])

                    sw = ExitStack()
                    sig = [sb(sw, "sig%d" % i, [128, 512], F32) for i in range(4)]
                    bcs = [sb(sw, "bcs%d" % i, [128, 512], F32) for i in range(4)]
                    sig += [sb(sw, "sig_s%d" % i, [128, NS], F32) for i in range(2)]
                    bcs += [sb(sw, "bcs_s%d" % i, [128, NS], F32) for i in range(2)]
                    g1 = [sb(sw, "g1_%d" % i, [128, 512], F32) for i in range(2)]
                    g2 = [sb(sw, "g2_%d" % i, [128, 512], F32) for i in range(2)]
                    wnames = ["sig%d" % i for i in range(6)] + ["bcs%d" % i for i in range(6)] + ["g10", "g11", "g20", "g21"]
                    for n_ in wnames:
                        B[n_] = k.buf(n_)
                    units = [(4, 5), (2, 3), (0, 1), (12, 13), (6, 7), (14, 15), (8,), (16, 17), (18,)]
                    sig_of = {}
                    bcs_of = {}
                    sig_live = [set(), set(), set()]
                    bcs_live = [set(), set(), set()]

                    def take(live, bi):
                        base = 2 * bi if bi < 2 else 4
                        for sl in (base, base + 1):
                            if sl not in live[bi]:
                                live[bi].add(sl)
                                return sl
                        raise RuntimeError("no free gate buffer")
                    for un in units:
                        work = []
                        ri = rr("rin", NRIN)
                        nc_ = len(un) * 128
                        k.dma("pool", ring_in[ri][:, :, 0:nc_],
                              w_in[l][:, un[0] * 128:un[0] * 128 + nc_].rearrange("(k p) c -> p k c", p=128),
                              writes=[b_rin[ri]])
                        cs = sorted(un, key=lambda c: (0 if (c <= 2 or 6 <= c <= 8) else 1, c))
                        for c in cs:
                            work.append((c, ri, (c - un[0]) * 128))
                        if un == units[0]:
                            work0 = work
                            continue
                        if un == units[1]:
                            it = [(w_, wb) for wb in wblocks for w_ in (work0 + work)]
                        else:
                            it = [(w_, wb) for w_ in work for wb in wblocks]
                        for ((c, ri, coff), (c0, n, bi)) in it:
                            if True:
                                cols = slice(c0, c0 + n)
                                pi = nextps()
                                mm_group(pi, n, [(ring_in[ri][:, kk, coff:coff + 128], hy[:, kk, cols]) for kk in range(8)],
                                         [b_rin[ri]] + hyb(bi))
                                P = ps[pi][:, 0:n]
                                bp = b_ps[pi]
                                if 3 <= c <= 5:
                                    si = take(sig_live, bi)
                                    sig_of[(c - 3, bi)] = si
                                    act(AF.Sigmoid, sig[si][:, 0:n], P, [bp], [B["sig%d" % si]])
                                elif c <= 2:
                                    j = c
                                    si = sig_of[(j, bi)]
                                    sig_live[bi].discard(si)
                                    if bi < 2:
                                        tt(hist_a[:, j, KA - 1 + c0:KA - 1 + c0 + n], P, sig[si][:, 0:n], ALU.mult,
                                           [bp, B["sig%d" % si]], [B["hist_a"]])
                                        if half == 1 and bi == 1:
                                            tt(tail_a[:, j, :], ps[pi][:, n - (KA - 1):n], sig[si][:, n - (KA - 1):n],
                                               ALU.mult, [bp, B["sig%d" % si]], [B["tail"]])
                                    else:
                                        tt(has[:, j, :, KA - 1], P, sig[si][:, 0:n], ALU.mult,
                                           [bp, B["sig%d" % si]], [B["has"]])
                                elif 12 <= c <= 14:
                                    si = take(bcs_live, bi)
                                    bcs_of[(c - 12, bi)] = si
                                    acopy(bcs[si][:, 0:n], P, [bp], [B["bcs%d" % si]])
                                elif 6 <= c <= 8:
                                    j = c - 6
                                    si = bcs_of[(j, bi)]
                                    bcs_live[bi].discard(si)
                                    if bi < 2:
                                        tt(hist_b[:, j, KB - 1 + c0:KB - 1 + c0 + n], P, bcs[si][:, 0:n], ALU.mult,
                                           [bp, B["bcs%d" % si]], [B["hist_b"]])
                                        if half == 1 and bi == 1:
                                            tt(tail_b[:, j, :], ps[pi][:, n - (KB - 1):n], bcs[si][:, n - (KB - 1):n],
                                               ALU.mult, [bp, B["bcs%d" % si]], [B["tail"]])
                                    else:
                                        tt(hbs[:, j, :, KB - 1], P, bcs[si][:, 0:n], ALU.mult,
                                           [bp, B["bcs%d" % si]], [B["hbs"]])
                                else:
                                    gi = r2("g")
                                    G1, G2 = g1[gi][:, 0:n], g2[gi][:, 0:n]
                                    bg1, bg2 = B["g1%d" % gi], B["g2%d" % gi]
                                    if c <= 16:
                                        act(AF.Gelu_apprx_tanh, u_t[:, c - 15, cols], P, [bp], [B["u"]])
                                    else:
                                        act(AF.Gelu_apprx_tanh, gv[:, c - 17, cols], P, [bp], [B["gv"]])
                            build_diag(1)
                    build_diag(1000)
                    if half == 0:
                        k.dma("sp", dscr[l], diag[:].rearrange("p j t m -> p (j t m)"), reads=[B["diag0"], B["diag1"], B["diag2"]], writes=[b_dscr[l]])
                    k.dma("pool", wbb[:], w_in[l][:, 9 * 128:12 * 128].rearrange("(k p) c -> p k c", p=128),
                          writes=b_wbb)

                    k.retire([B[n_] for n_ in wnames])
                    sw.close()
                    for g_ in range(2):
                        k.dma("pool", rfi0[:, :, g_, :],
                              w_ffn_in[l][:, g_ * DFF:g_ * DFF + 128].rearrange("(k p) c -> p k c", p=128), writes=[b_rfi0])
                    if half == 0 and l == 0:
                        finish_wm(0)
                    for m2 in range(4):
                        k.dma("pool", ring_o[m2][:], w_o[l][:, m2 * 256:(m2 + 1) * 256].rearrange("(k p) c -> p k c", p=128),
                              writes=[b_ro[m2]])

                    def wo_emit(wb, ms):
                        c0, n, bi = wb
                        cols = slice(c0, c0 + n)
                        for m in ms:
                            m2, mm_ = m // 2, m % 2
                            pi = nextps()
                            mm_group(pi, n, [(ring_o[m2][:, kk, mm_ * 128:(mm_ + 1) * 128], hy[:, kk, cols])
                                             for kk in range(8)], [b_ro[m2]] + hyb(bi))
                            tt(xT[:, m, cols], ps[pi][:, 0:n], xT[:, m, cols], ALU.add, [b_ps[pi], b_x[bi]], [b_x[bi]])

                    v32 = [sb(sm, "v32_%d" % i, [128, 3, 256], F32) for i in range(2)]
                    vbf = sb(sm, "vbf", [128, 5, 256], BF16)
                    sqbf = sb(sm, "sqbf", [128, 5, 256], BF16)
                    mean = sb(sm, "mean", [128, 2, 256], F32)
                    mrC = sb(sm, "mrC", [128, 2, NS], F32)
                    rstd = sb(sm, "rstd", [128, 2, 256], F32)
                    tA = [sb(sm, "tA%d" % i, [128, 256], F32) for i in range(3)]
                    tS = [sb(sm, "tS%d" % i, [128, 256], F32) for i in range(3)]
                    tN = [sb(sm, "tN%d" % i, [128, 256], F32) for i in range(3)]
                    tT = [sb(sm, "tT%d" % i, [128, 256], F32) for i in range(2)]
                    cbs = [sb(sm, "cbs%d" % i, [128, 256], F32) for i in range(2)]
                    vn32 = sb(sm, "vn32", [128, 2, 256], F32)
                    vntok2 = [sb(sm, "vntok%d" % i, [128, 2, 256], BF16) for i in range(2)]
                    vntok = vntok2[0]
                    tC = sb(sm, "tC", [128, 256], F32)
                    bst2 = [sb(sm, "bst%d" % i, [128, 2, 6], F32) for i in range(2)]
                    mv2 = [sb(sm, "mv%d" % i, [128, 2, 2], F32) for i in range(2)]
                    rs22 = [sb(sm, "rs2%d" % i, [128, 4], F32) for i in range(2)]
                    for n_ in ["bst0", "bst1", "mv0", "mv1", "rs20", "rs21", "vntokB"]:
                        B[n_] = k.buf(n_)
                    mix_bufs = mix_bufs + [B[n_] for n_ in ["bst0", "bst1", "mv0", "mv1", "rs20", "rs21", "vntokB"]]
                    pnames = ["v320", "v321", "vbf", "sqbf", "mean0", "mean1", "mean2", "rstd0", "rstd1", "rstd2", "tA0", "tA1", "tA2", "tS0", "tS1", "tS2",
                              "tN0", "tN1", "tN2", "tT0", "tT1", "cbs0", "cbs1", "vn32", "vntok", "tC", "ybt0", "ybt1"]
                    for n_ in pnames:
                        B[n_] = k.buf(n_)
                    mix_bufs = mix_bufs + [B[n_] for n_ in pnames]
                    S = {}

                    def st_CA(s):
                        c0, n, sbi, sample = sblocks[s]
                        conv = []
                        if not sample:
                            for j in range(3):
                                pi = nextps()
                                mm_group(pi, n, [(diag[:, j, kk, :], hist_a[:, j, c0 + kk:c0 + kk + n]) for kk in range(KA)],
                                         [B["diag%d" % j], B["hist_a"]])
                                conv.append((ps[pi][:, 0:n], b_ps[pi]))
                        else:
                            for j in range(3):
                                tt(prod[:], has[:, j], pAB[:, l, j, 0:KA].unsqueeze(1).broadcast_to([128, NS, KA]),
                                   ALU.mult, [B["has"], b_pAB], [B["prod"]])
                                k.op("dve", lambda e, j=j: e.tensor_reduce(out=cvs[:, j, :], in_=prod[:], axis=AX.X,
                                                                          op=ALU.add), [B["prod"]], [B["cvs"]])
                                conv.append((cvs[:, j, :], B["cvs"]))
                        S[s] = conv

                    def st_EV(s):
                        c0, n, sbi, sample = sblocks[s]
                        cols = slice(c0, c0 + n)
                        for j in range(3):
                            src, bsrc = S[s][j]
                            bias = pAB[:, l, j, 31:32]
                            act(AF.Identity, v32[s % 2][:, j, 0:n], src, [bsrc, b_pAB], [B["v32%d" % (s % 2)]], bias=bias)
                            act(AF.Identity, vbf[:, j, 0:n], src, [bsrc, b_pAB], [B["vbf"]], bias=bias)
                            act(AF.Square, sqbf[:, j, 0:n], src, [bsrc, b_pAB], [B["sqbf"]], bias=bias)
                        for q in range(2 if sample else 0):
                            act(AF.Identity, vbf[:, 3 + q, 0:n], gv[:, q, cols], [B["gv"]], [B["vbf"]])
                            act(AF.Square, sqbf[:, 3 + q, 0:n], gv[:, q, cols], [B["gv"]], [B["sqbf"]])

                    def ln_stats(n, nch, off, invn, mslot):
                        pi = nextps()
                        mm_group(pi, n, [(ones_b[:], vbf[:, off + j, 0:n]) for j in range(nch)], [B["vbf"], b_const])
                        fns = []
                        for j in range(nch):
                            fns.append(lambda e, j=j: e.matmul(ps[pi][:, 256:256 + n], lhsT=ones_b[:],
                                                              rhs=sqbf[:, off + j, 0:n], start=(j == 0),
                                                              stop=(j == nch - 1)))
                        k.group("pe", fns, reads=[B["sqbf"], b_const], writes=[b_ps[pi]])
                        ti = r2("t")
                        T = tT[ti][:, 0:n]
                        bT = B["tT%d" % ti]
                        M_ap = mean[:, mslot, 0:n] if mslot < 2 else mrC[:, 0, 0:n]
                        R_ap = rstd[:, mslot, 0:n] if mslot < 2 else mrC[:, 1, 0:n]
                        act(AF.Identity, M_ap, ps[pi][:, 0:n], [b_ps[pi]], [B["mean%d" % mslot]], scale=invn)
                        act(AF.Square, T, ps[pi][:, 0:n], [b_ps[pi]], [bT], scale=invn)
                        stt(T, ps[pi][:, 256:256 + n], invn, T, ALU.mult, ALU.subtract, [b_ps[pi], bT], [bT])
                        act(AF.Ln, T, T, [bT, b_const], [bT], bias=epsc[:, 0:1])
                        act(AF.Exp, R_ap, T, [bT], [B["rstd%d" % mslot]], scale=-0.5)

                    def st_ST(s):
                        c0, n, sbi, sample = sblocks[s]
                        if sample:
                            ln_stats(n, 2, 3, 1.0 / CC, 2)
                        ln_stats(n, 3, 0, 1.0 / CA, s % 2)

                    def st_BB(s):
                        c0, n, sbi, sample = sblocks[s]
                        cols = slice(c0, c0 + n)
                        pbs = []
                        for j in range(3):
                            pb = nextps()
                            mm_group(pb, n, [(wbb[:, kk, j * 128:(j + 1) * 128], hy[:, kk, cols]) for kk in range(8)],
                                     [b_wbb[j], b_hy[sbi]])
                            pbs.append(pb)
                        for j in range(3):
                            ci = r2("cbs")
                            if not sample:
                                pi = nextps()
                                mm_group(pi, n, [(diag[:, j, KA + kk, :], hist_b[:, j, c0 + kk:c0 + kk + n])
                                                 for kk in range(KB)], [B["diag%d" % j], B["hist_b"]])
                                acopy(cbs[ci][:, 0:n], ps[pi][:, 0:n], [b_ps[pi]], [B["cbs%d" % ci]])
                                cb_ap, b_cb = cbs[ci][:, 0:n], B["cbs%d" % ci]
                            else:
                                tt(prod[:, :, 0:KB], hbs[:, j], pAB[:, l, j, 34:37].unsqueeze(1).broadcast_to([128, NS, KB]),
                                   ALU.mult, [B["hbs"], b_pAB], [B["prod"]])
                                k.op("dve", lambda e, j=j: e.tensor_reduce(out=cvb[:, j, :], in_=prod[:, :, 0:KB],
                                                                          axis=AX.X, op=ALU.add), [B["prod"]], [B["cvb"]])
                                cb_ap, b_cb = cvb[:, j, :], B["cvb"]
                            tt(hy[:, 3 + j, cols], ps[pbs[j]][:, 0:n], cb_ap, ALU.mult, [b_ps[pbs[j]], b_cb], [b_hy[sbi]])

                    def st_LN(s):
                        c0, n, sbi, sample = sblocks[s]
                        cols = slice(c0, c0 + n)
                        for q in range(2):
                            ti = r2("t")
                            TA = tA[ti][:, 0:n]
                            bA = B["tA%d" % ti]
                            tt(TA, gv[:, q, cols], mrC[:, 0, 0:n], ALU.subtract, [B["gv"], B["mean2"]], [bA])
                            tt(TA, TA, mrC[:, 1, 0:n], ALU.mult, [bA, B["rstd2"]], [bA])
                            act(AF.Identity, vn32[:, q, 0:n], TA, [bA, b_pC], [B["vn32"]],
                                scale=pC[:, l, q, 0:1], bias=pC[:, l, q, 1:2])

                    def st_LNA(s):
                        c0, n, sbi, sample = sblocks[s]
                        cols = slice(c0, c0 + n)
                        V = v32[s % 2]
                        bV = B["v32%d" % (s % 2)]
                        for j in range(3):
                            TA, bA = tA[j][:, 0:n], B["tA%d" % j]
                            tt(TA, V[:, j, 0:n], mean[:, s % 2, 0:n], ALU.subtract, [bV, B["mean%d" % (s % 2)]], [bA])
                            tt(TA, TA, rstd[:, s % 2, 0:n], ALU.mult, [bA, B["rstd%d" % (s % 2)]], [bA])
                        for j in range(3):
                            TA, bA = tA[j][:, 0:n], B["tA%d" % j]
                            act(AF.Silu, hy[:, j, cols], TA, [bA, b_pAB], [b_hy[sbi]],
                                scale=pAB[:, l, j, 32:33], bias=pAB[:, l, j, 33:34])

                    def st_C1(s):
                        c0, n, sbi, sample = sblocks[s]
                        z = s % 2
                        bst, mv, rs2, VT = bst2[z], mv2[z], rs22[z], vntok2[z]
                        bB, bM, bR, bV = B["bst%d" % z], B["mv%d" % z], B["rs2%d" % z], (B["vntok"] if z == 0 else B["vntokB"])
                        pi = nextps()
                        transposes([(ps[pi][:, ti_ * 256 + q * 128: ti_ * 256 + (q + 1) * 128],
                                     gv[:, q, c0 + ti_ * 128:c0 + (ti_ + 1) * 128], ident_f[:])
                                    for ti_ in range(2) for q in range(2)], [B["gv"], b_const], [b_ps[pi]])
                        for ti_ in range(2):
                            k.op("dve", lambda e, ti_=ti_: e.bn_stats(out=bst[:, ti_, :], in_=ps[pi][:, ti_ * 256:(ti_ + 1) * 256]),
                                 [b_ps[pi]], [bB])
                            k.op("dve", lambda e, ti_=ti_: e.bn_aggr(out=mv[:, ti_, :], in_=bst[:, ti_, :]), [bB], [bM])
                        k.op("pool", lambda e: e.tensor_scalar(out=rs2[:, 0:2], in0=mv[:, :, 1], scalar1=1.0, scalar2=EPS,
                                                               op0=ALU.mult, op1=ALU.add), [bM], [bR])
                        k.op("pool", lambda e: e.tensor_tensor(out=rs2[:, 2:4], in0=rs2[:, 0:2], in1=negh[:], op=ALU.pow),
                             [bR, b_const], [bR])
                        for ti_ in range(2):
                            tsc(VT[:, ti_, :], ps[pi][:, ti_ * 256:(ti_ + 1) * 256], mv[:, ti_, 0:1],
                                rs2[:, 2 + ti_:3 + ti_], ALU.subtract, ALU.mult, [b_ps[pi], bM, bR], [bV])

                    def st_C2(s):
                        c0, n, sbi, sample = sblocks[s]
                        cols = slice(c0, c0 + n)
                        z = s % 2
                        VT = vntok2[z]
                        bV = B["vntok"] if z == 0 else B["vntokB"]
                        for q in range(2):
                            pi = nextps()
                            fns = []
                            for ti_ in range(2):
                                for hh in range(2):
                                    h = 2 * q + hh
                                    fns.append(lambda e, ti_=ti_, hh=hh, h=h, pi=pi: e.matmul(
                                        ps[pi][hh * 64:(hh + 1) * 64, ti_ * 128:(ti_ + 1) * 128],
                                        lhsT=VT[:, ti_, h * 64:(h + 1) * 64], rhs=WmT[:, l, h, :],
                                        start=True, stop=True))
                            k.group("pe", fns, reads=[bV, b_WmT], writes=[b_ps[pi]])
                            stt(tC[:, 0:256].rearrange("p (t s) -> p t s", t=2),
                                ps[pi][:, 0:256].rearrange("p (t s) -> p t s", t=2), pC[:, l, q, 0:1],
                                bias2[:, l, q, :].unsqueeze(1).broadcast_to([128, 2, 128]), ALU.mult, ALU.add,
                                [b_ps[pi], b_pC, b_bias2], [B["tC"]])
                            tt(hy[:, 6 + q, cols], tC[:, 0:256], u_t[:, q, cols], ALU.mult, [B["tC"], B["u"]],
                               [b_hy[sbi]])

                    def st_TR(s):
                        c0, n, sbi, sample = sblocks[s]
                        if not sample:
                            pi = nextps()
                            transposes([(ps[pi][:, ti_ * 256 + q * 128: ti_ * 256 + (q + 1) * 128],
                                         vn32[:, q, ti_ * 128:(ti_ + 1) * 128], ident_f[:])
                                        for ti_ in range(2) for q in range(2)], [B["vn32"], b_const], [b_ps[pi]])
                            acopy(vntok[:].rearrange("p t c -> p (t c)"), ps[pi][:, 0:512], [b_ps[pi]], [B["vntok"]])
                        else:
                            pi = nextps()
                            transposes([(ps[pi][0:NS, q * 128:(q + 1) * 128], vn32[:, q, 0:NS], ident_f[:])
                                        for q in range(2)], [B["vn32"], b_const], [b_ps[pi]])
                            acopy(osb[0:NS, 0:256], ps[pi][0:NS, 0:256], [b_ps[pi]], [B["osb"]])
                            k.dma("sp", ncv[l], osb[0:NS, 0:256], reads=[B["osb"]])

                    def st_SP(s):
                        c0, n, sbi, sample = sblocks[s]
                        cols = slice(c0, c0 + n)
                        if not sample:
                            for q in range(2):
                                pi = nextps()
                                fns = []
                                for ti_ in range(2):
                                    for hh in range(2):
                                        h = 2 * q + hh
                                        fns.append(lambda e, ti_=ti_, hh=hh, h=h, pi=pi: e.matmul(
                                            ps[pi][hh * 64:(hh + 1) * 64, ti_ * 128:(ti_ + 1) * 128],
                                            lhsT=vntok[:, ti_, h * 64:(h + 1) * 64], rhs=WmT[:, l, h, :],
                                            start=True, stop=True))
                                k.group("pe", fns, reads=[B["vntok"], b_WmT], writes=[b_ps[pi]])
                                tt(tC[:, 0:256].rearrange("p (t s) -> p t s", t=2),
                                   ps[pi][:, 0:256].rearrange("p (t s) -> p t s", t=2),
                                   bsb[:, l, q, :].unsqueeze(1).broadcast_to([128, 2, 128]), ALU.add,
                                   [b_ps[pi], b_bsb], [B["tC"]])
                                tt(hy[:, 6 + q, cols], tC[:, 0:256], u_t[:, q, cols], ALU.mult, [B["tC"], B["u"]],
                                   [b_hy[sbi]])
                        else:
                            for q in range(2):
                                tsc(tC[:, 0:n], vn32[:, q, 0:n], w00[:, l, q:q + 1], bsb[:, l, q, 0:1], ALU.mult, ALU.add,
                                    [B["vn32"], b_bsb], [B["tC"]])
                                tt(hy[:, 6 + q, cols], tC[:, 0:n], u_t[:, q, cols], ALU.mult, [B["tC"], B["u"]],
                                   [b_hy[sbi]])

                    NSB = len(sblocks)
                    wo_done = set()
                    st_CA(0)
                    st_C1(0)
                    st_EV(0); st_BB(0); st_ST(0)
                    for s in range(NSB):
                        nx = s + 1 < NSB
                        if nx:
                            st_CA(s + 1)
                        is_s = sblocks[s][3]
                        if nx and not sblocks[s + 1][3]:
                            st_C1(s + 1)
                        if is_s:
                            wo_emit(wblocks[0], range(0, 8))
                            wo_emit(wblocks[1], range(0, 3))
                            st_ST(s)
                            wo_emit(wblocks[1], range(3, 8))
                            wo_done.add(1)
                            st_LN(s)
                            st_TR(s)
                        if nx:
                            st_EV(s + 1)
                            st_BB(s + 1)
                        if nx and not sblocks[s + 1][3]:
                            st_ST(s + 1)
                        st_LNA(s)
                        if not is_s:
                            st_C2(s)
                        if s == 3 and half == 0:
                            wo_emit(wblocks[0], range(0, 8))
                        if is_s:
                            st_SP(s)

                    if half == 1:
                        pi = nextps()
                        transposes([(ps[pi][0:KA - 1, j * 128:(j + 1) * 128], tail_a[:, j, :], ident_f[:]) for j in range(3)],
                                   [B["tail"], b_const], [b_ps[pi]])
                        acopy(osb[0:KA - 1, :], ps[pi][0:KA - 1, 0:384], [b_ps[pi]], [B["osb"]])
                        k.dma("sp", ncap[l], osb[0:KA - 1, :], reads=[B["osb"]])
                        pi = nextps()
                        transposes([(ps[pi][0:KB - 1, j * 128:(j + 1) * 128], tail_b[:, j, :], ident_f[:]) for j in range(3)],
                                   [B["tail"], b_const], [b_ps[pi]])
                        acopy(osb[0:KB - 1, :], ps[pi][0:KB - 1, 0:384], [b_ps[pi]], [B["osb"]])
                        k.dma("sp", ncbp[l], osb[0:KB - 1, :], reads=[B["osb"]])
                        pi = nextps()
                        transposes([(ps[pi][0:NS, j * 128:(j + 1) * 128], has[:, j, :, KA - 1], ident_f[:]) for j in range(3)],
                                   [B["has"], b_const], [b_ps[pi]])
                        acopy(osb[0:NS, :], ps[pi][0:NS, 0:384], [b_ps[pi]], [B["osb"]])
                        k.dma("sp", ncas[l, :, KA - 2, :], osb[0:NS, :], reads=[B["osb"]])
                        pi = nextps()
                        transposes([(ps[pi][0:NS, j * 128:(j + 1) * 128], hbs[:, j, :, KB - 1], ident_f[:]) for j in range(3)],
                                   [B["hbs"], b_const], [b_ps[pi]])
                        acopy(osb[0:NS, :], ps[pi][0:NS, 0:384], [b_ps[pi]], [B["osb"]])
                        k.dma("sp", ncbs[l, :, KB - 2, :], osb[0:NS, :], reads=[B["osb"]])
                    else:
                        acopy(carry_a[:, l], hist_a[:, :, HALF:HALF + KA - 1], [B["hist_a"]], [b_carry])
                        acopy(carry_b[:, l], hist_b[:, :, HALF:HALF + KB - 1], [B["hist_b"]], [b_carry])

                    if half == 0 and l == 0:
                        finish_vec(1)
                        finish_wm(1)
                    for (c0, n, bi) in wblocks[1:]:
                        if bi not in wo_done:
                            wo_emit((c0, n, bi), range(8))
                    k.retire(mix_bufs + b_rin + b_ro)
                    k.retire(b_wbb, reads_only=True)
                if half == 0 and l == 0:
                    s0.close()

                with ExitStack() as sf:
                    gT = sb(sf, "gT", [128, NJ, NT], BF16)
                    fs = [sb(sf, "fs%d" % i, [128, 512], F32) for i in range(2)]
                    b_gT = k.buf("gT")
                    b_fs = [k.buf("fs%d" % i) for i in range(2)]
                    ring_fi = [sb(sf, "rfi%d" % i, [128, 8, 2, 512], BF16) for i in range(3)]
                    b_rfi = [k.buf("rfi%d" % i) for i in range(3)]
                    ring_fo = [sb(sf, "rfo%d" % i, [128, NJ, 256], BF16) for i in range(2)]
                    b_rfo = [k.buf("rfo%d" % i) for i in range(2)]
                    for (c0, n, bi) in wblocks:
                        rmsnorm(c0, n, b_x[bi], 2 + l, lambda kk, c0=c0, n=n: hy[:, kk, c0:c0 + n], hyb(bi))
                    fsi = 0
                    for j4 in [0, 1, 5, 9, 13, 17, 21]:
                        if j4 == 0:
                            nj = 1
                            RF, bRF = rfi0, b_rfi0
                        else:
                            nj = min(4, NJ - j4)
                            ri = rr("rfi", 3)
                            RF, bRF = ring_fi[ri], b_rfi[ri]
                            for g_ in range(2):
                                k.dma("pool", RF[:, :, g_, 0:nj * 128],
                                      w_ffn_in[l][:, g_ * DFF + j4 * 128:g_ * DFF + (j4 + nj) * 128].rearrange("(k p) c -> p k c", p=128),
                                      writes=[bRF])
                        if j4 == 0:
                            first = (RF, bRF)
                            continue
                        if j4 == 1:
                            it = [(j_, jj_, R_, wb) for wb in wblocks
                                  for (j_, jj_, R_) in [(0, 0, first)] + [(j4 + q_, q_, (RF, bRF)) for q_ in range(nj)]]
                        else:
                            it = [(j4 + jj, jj, (RF, bRF), wb) for jj in range(nj) for wb in wblocks]
                        for (j, jj, (RFx, bRFx), (c0, n, bi)) in it:
                            cols = slice(c0, c0 + n)
                            pg = nextps()
                            mm_group(pg, n, [(RFx[:, kk, 0, jj * 128:(jj + 1) * 128], hy[:, kk, cols])
                                             for kk in range(8)], [bRFx] + hyb(bi))
                            pu = nextps()
                            mm_group(pu, n, [(RFx[:, kk, 1, jj * 128:(jj + 1) * 128], hy[:, kk, cols])
                                             for kk in range(8)], [bRFx] + hyb(bi))
                            fsi ^= 1
                            FS = fs[fsi][:, 0:n]
                            act(AF.Sigmoid, FS, ps[pg][:, 0:n], [b_ps[pg]], [b_fs[fsi]])
                            tt(FS, ps[pg][:, 0:n], FS, ALU.mult, [b_ps[pg], b_fs[fsi]], [b_fs[fsi]])
                            tt(gT[:, j, cols], ps[pu][:, 0:n], FS, ALU.mult, [b_ps[pu], b_fs[fsi]], [b_gT])
                    for m2 in range(4):
                        ri = rr("rfo", 2)
                        k.dma("pool", ring_fo[ri][:],
                              w_ffn_out[l][:, m2 * 256:(m2 + 1) * 256].rearrange("(j p) c -> p j c", p=128),
                              writes=[b_rfo[ri]])
                        if m2 == 3:
                            it = [(mm_, wb) for wb in wblocks for mm_ in range(2)]
                        else:
                            it = [(mm_, wb) for mm_ in range(2) for wb in wblocks]
                        for (mm_, (c0, n, bi)) in it:
                            m = 2 * m2 + mm_
                            cols = slice(c0, c0 + n)
                            pi = nextps()
                            mm_group(pi, n, [(ring_fo[ri][:, j, mm_ * 128:(mm_ + 1) * 128], gT[:, j, cols])
                                             for j in range(NJ)], [b_rfo[ri], b_gT])
                            tt(xT[:, m, cols], ps[pi][:, 0:n], xT[:, m, cols], ALU.add, [b_ps[pi], b_x[bi]], [b_x[bi]])
                    k.retire([b_gT] + b_fs + b_rfi + b_rfo)

            if half == 0:
                s1h1 = ExitStack()
                gfb0 = sb(s1h1, "gfb0", [128, 1024], F32)
                b_gfb0 = k.buf("gfb0")
                k.dma("sp", gfb0[:], norm_final_g.partition_broadcast(128), writes=[b_gfb0])
                xin1 = [sb(s1h1, "xin%d" % i, [128, 1024], F32) for i in range(8)]
                b_xin1 = [k.buf("xin%d" % i) for i in range(8)]
                for ti in range(8):
                    r0 = HALF + ti * 128
                    k.dma("sp", xin1[ti][:], xp[r0:r0 + 128, :], writes=[b_xin1[ti]])
                xsn1 = sb(s1h1, "xsn", [NS, 1024], F32)
                b_xsn1 = k.buf("xsn")
                k.dma("sp", xsn1[:], xs, writes=[b_xsn1])

            with ExitStack() as s2:
                ost = [sb(s2, "ost%d" % i, [128, 1024], F32) for i in range(3)]
                b_ost = [k.buf("ost%d" % i) for i in range(3)]
                junk = sb(s2, "junk", [128, 512], BF16)
                b_junk = k.buf("junk")
                ssq = [sb(s2, "ssq%d" % i, [128, 8], F32) for i in range(3)]
                b_ssq = [k.buf("ssq%d" % i) for i in range(3)]
                if half == 0:
                    gfb, b_gfb = gfb0, b_gfb0
                else:
                    gfb = sb(s2, "gfb", [128, 1024], F32)
                    b_gfb = k.buf("gfb")
                    k.dma("sp", gfb[:], norm_final_g.partition_broadcast(128), writes=[b_gfb])
                tiles = [(ti * 128, 128, ti // 4) for ti in range(8)] + ([(HALF, NS, 2)] if half == 1 else [])
                groups = [tiles[i:i + 2] for i in range(0, 8, 2)] + ([[tiles[8]]] if half == 1 else [])
                oidx = 0
                for gi, grp in enumerate(groups):
                    sq, bsq = ssq[gi % 3], b_ssq[gi % 3]
                    tn = grp[0][1]
                    ng = len(grp)
                    banks = []
                    for (c0, tn_, bi) in grp:
                        for hh in range(2):
                            pi = nextps()
                            transposes([(ps[pi][0:tn, c * 128:(c + 1) * 128], xT[:, 4 * hh + c, c0:c0 + tn], ident_f[:])
                                        for c in range(4)], [b_x[bi], b_const], [b_ps[pi]])
                            banks.append(pi)
                    for q_, pi in enumerate(banks):
                        k.op("act", lambda e, q_=q_, pi=pi: e.activation(out=junk[0:tn, :], in_=ps[pi][0:tn, :], func=AF.Square,
                                                                         accum_out=sq[0:tn, q_:q_ + 1]),
                             [b_ps[pi]], [b_junk, bsq])
                    sv = sq[0:tn, 0:2 * ng].rearrange("p (t h) -> p t h", h=2)
                    tt(sq[0:tn, 4:4 + ng], sv[:, :, 0], sv[:, :, 1], ALU.add, [bsq], [bsq])
                    act(AF.Ln, sq[0:tn, 4:4 + ng], sq[0:tn, 4:4 + ng], [bsq, b_const], [bsq], scale=1.0 / D, bias=epsc[0:tn, 0:1])
                    act(AF.Exp, sq[0:tn, 6:6 + ng], sq[0:tn, 4:4 + ng], [bsq], [bsq], scale=-0.5)
                    for t_, (c0, tn_, bi) in enumerate(grp):
                        oi = oidx % 3
                        oidx += 1
                        for hh in range(2):
                            pi = banks[2 * t_ + hh]
                            stt(ost[oi][0:tn, hh * 512:(hh + 1) * 512], ps[pi][0:tn, :], sq[0:tn, 6 + t_:7 + t_],
                                gfb[0:tn, hh * 512:(hh + 1) * 512], ALU.mult, ALU.mult, [b_ps[pi], bsq, b_gfb], [b_ost[oi]])
                        if bi < 2:
                            r0 = half * HALF + c0
                            k.dma("sp", yp[r0:r0 + 128, :], ost[oi][:], reads=[b_ost[oi]])
                        else:
                            k.dma("sp", ys, ost[oi][0:NS, :], reads=[b_ost[oi]])
                k.retire(b_ost + b_ssq + [b_junk, b_gfb])


        k.finish()
    return nc


_W_NAMES = ["norm_mix_g", "w_in", "dw_a", "dw_a_bias", "ln_a_g", "ln_a_b", "conv_b_w", "ln_c_g", "ln_c_b",
            "w_s", "b_s", "w_o", "norm_ffn_g", "w_ffn_in", "w_ffn_out", "norm_final_g"]


def kernel(**inputs):
    f = lambda a: np.ascontiguousarray(np.asarray(a, dtype=np.float32))
    x_prompt = f(inputs["x_prompt"])
    x_sample = f(inputs["x_sample"])
    sca = f(inputs["state_conv_a"])
    scb = f(inputs["state_conv_b"])
    weights = {n: f(inputs[n]) for n in _W_NAMES}
    nc = build_program()
    in_maps = []
    for c in range(8):
        m = dict(weights)
        m["xp"] = x_prompt[c]
        m["xs"] = np.ascontiguousarray(x_sample[c * NS:(c + 1) * NS, 0, :])
        m["sca"] = np.ascontiguousarray(sca[:, c * NS:(c + 1) * NS])
        m["scb"] = np.ascontiguousarray(scb[:, c * NS:(c + 1) * NS])
        in_maps.append(m)
    res = run_bass_kernel_spmd(nc, in_maps, core_ids=list(range(8)))
    R = res.results
    y_prompt = np.stack([np.asarray(R[c]["yp"], dtype=np.float32) for c in range(8)], axis=0)
    y_sample = np.concatenate([np.asarray(R[c]["ys"], dtype=np.float32) for c in range(8)], axis=0)[:, None, :]
    ncap = np.stack([np.asarray(R[c]["ncap"], dtype=np.float32) for c in range(8)], axis=1)
    ncbp = np.stack([np.asarray(R[c]["ncbp"], dtype=np.float32) for c in range(8)], axis=1)
    ncas = np.concatenate([np.asarray(R[c]["ncas"], dtype=np.float32) for c in range(8)], axis=1)
    ncbs = np.concatenate([np.asarray(R[c]["ncbs"], dtype=np.float32) for c in range(8)], axis=1)
    ncv = np.concatenate([np.asarray(R[c]["ncv"], dtype=np.float32) for c in range(8)], axis=1)[:, :, None, :]
    return (y_prompt, y_sample, ncap, ncbp, ncas, ncbs, ncv)
```

```python
from contextlib import ExitStack

import numpy as np
import concourse.bass as bass
import concourse.mybir as mybir
from concourse.bass_utils import run_bass_kernel_spmd

F32 = mybir.dt.float32
BF16 = mybir.dt.bfloat16
AF = mybir.ActivationFunctionType
ALU = mybir.AluOpType
AX = mybir.AxisListType

D = 1024
SEQ = 2048
NS = 16
CA = 384
CB = 384
CC = 256
KA = 31
KB = 3
DFF = 2816
NJ = DFF // 128
DIN = 2432
EPS = 1e-6
HALF = 1024
GELU_C = 1.5957691216057308


class Sem:
    def __init__(self, h, name):
        self.h = h
        self.name = name
        self.val = 0


class Buf:
    __slots__ = ("name", "w", "r")

    def __init__(self, name, pending=None):
        self.name = name
        self.w = dict(pending) if pending else {}
        self.r = {}


def _merge(dst, s, v):
    if dst.get(s, 0) < v:
        dst[s] = v


class K:
    def __init__(self, nc, stack, n_dma_sems=28):
        self.nc = nc
        self.eng = {"pe": nc.tensor, "act": nc.scalar, "dve": nc.vector, "pool": nc.gpsimd, "sp": nc.sync}
        self.esem = {e: Sem(stack.enter_context(nc.semaphore("c_" + e)), "c_" + e) for e in self.eng}
        self.seen = {e: {} for e in self.eng}
        self.dsems = [Sem(stack.enter_context(nc.semaphore("d%d" % i)), "d%d" % i) for i in range(n_dma_sems)]
        self.dnext = 0
        self.pending = {}
        self.nwaits = 0
        self.ninstr = 0

    def buf(self, name):
        return Buf(name, self.pending)

    def retire(self, bufs, reads_only=False):
        for b in bufs:
            if not reads_only:
                for s, v in b.w.items():
                    _merge(self.pending, s, v)
            for s, v in b.r.items():
                _merge(self.pending, s, v)

    def _deps(self, e, reads, writes):
        deps = {}
        for b in reads:
            for s, v in b.w.items():
                _merge(deps, s, v)
        for b in writes:
            for s, v in b.w.items():
                _merge(deps, s, v)
            for s, v in b.r.items():
                _merge(deps, s, v)
        if e == "pe":
            deps.pop(self.esem["pe"], None)
        return deps

    def _wait(self, e, deps):
        seen = self.seen[e]
        eng = self.eng[e]
        for s, v in deps.items():
            if seen.get(s, 0) >= v:
                continue
            eng.wait_ge(s.h, v)
            seen[s] = v
            self.nwaits += 1

    def _commit(self, tok, reads, writes):
        s, v = tok
        for b in writes:
            _merge(b.w, s, v)
        for b in reads:
            _merge(b.r, s, v)

    def op(self, e, fn, reads=(), writes=()):
        self._wait(e, self._deps(e, reads, writes))
        ins = fn(self.eng[e])
        s = self.esem[e]
        ins.then_inc(s.h, 1)
        s.val += 1
        self.ninstr += 1
        self._commit((s, s.val), reads, writes)

    def group(self, e, fns, reads=(), writes=()):
        self._wait(e, self._deps(e, reads, writes))
        ins = None
        for fn in fns:
            ins = fn(self.eng[e])
            self.ninstr += 1
        s = self.esem[e]
        ins.then_inc(s.h, 1)
        s.val += 1
        self._commit((s, s.val), reads, writes)

    def slot_sem(self, stack, name):
        return Sem(stack.enter_context(self.nc.semaphore(name)), name)

    def dma(self, e, out, in_, reads=(), writes=(), slot=None, **kw):
        self._wait(e, self._deps(e, reads, writes))
        if False and slot is not None:
            d = slot
            if d.val > 0:
                self.eng[e].sem_clear(d.h)
                for b in writes:
                    b.w.pop(d, None)
                for e2 in self.seen:
                    self.seen[e2].pop(d, None)
                d.val = 0
            self.eng[e].dma_start(out=out, in_=in_, **kw).then_inc(d.h, 16)
            d.val = 16
            self.ninstr += 1
            self._commit((d, 16), reads, writes)
            return
        d = self.dsems[self.dnext]
        self.dnext = (self.dnext + 1) % len(self.dsems)
        if d.val > 0:
            self._wait(e, {d: d.val})
        self.eng[e].dma_start(out=out, in_=in_, **kw).then_inc(d.h, 16)
        d.val += 16
        self.ninstr += 1
        self._commit((d, d.val), reads, writes)

    def finish(self):
        deps = {}
        for d in self.dsems:
            if d.val > 0:
                deps[d] = d.val
        for e, s in self.esem.items():
            if s.val > 0:
                deps[s] = s.val
        self._wait("sp", deps)


def build_program():
    nc = bass.Bass("TRN2", target_bir_lowering=False)

    def din(name, shape):
        return nc.dram_tensor(name, list(shape), F32, kind="ExternalInput").ap()

    def dout(name, shape):
        return nc.dram_tensor(name, list(shape), F32, kind="ExternalOutput").ap()

    xp = din("xp", [SEQ, D])
    xs = din("xs", [NS, D])
    sca = din("sca", [2, NS, KA - 1, CA])
    scb = din("scb", [2, NS, KB - 1, CB])
    norm_mix_g = din("norm_mix_g", [2, D])
    w_in = din("w_in", [2, D, DIN])
    dw_a = din("dw_a", [2, KA, CA])
    dw_a_bias = din("dw_a_bias", [2, CA])
    ln_a_g = din("ln_a_g", [2, CA])
    ln_a_b = din("ln_a_b", [2, CA])
    conv_b_w = din("conv_b_w", [2, KB, CB])
    ln_c_g = din("ln_c_g", [2, CC])
    ln_c_b = din("ln_c_b", [2, CC])
    w_s = din("w_s", [2, 4, 128, 128])
    b_s = din("b_s", [2, 4, 128])
    w_o = din("w_o", [2, D, D])
    norm_ffn_g = din("norm_ffn_g", [2, D])
    w_ffn_in = din("w_ffn_in", [2, D, 2 * DFF])
    w_ffn_out = din("w_ffn_out", [2, DFF, D])
    norm_final_g = din("norm_final_g", [D])

    yp = dout("yp", [SEQ, D])
    ys = dout("ys", [NS, D])
    ncap = dout("ncap", [2, KA - 1, CA])
    ncbp = dout("ncbp", [2, KB - 1, CB])
    ncas = dout("ncas", [2, NS, KA - 1, CA])
    ncbs = dout("ncbs", [2, NS, KB - 1, CB])
    ncv = dout("ncv", [2, NS, CC])

    with ExitStack() as st:
        k = K(nc, st)

        uid = [0]

        def sb(stack, name, shape, dt):
            uid[0] += 1
            return stack.enter_context(nc.sbuf_tensor("%s_%d" % (name, uid[0]), list(shape), dt))

        dscr = nc.dram_tensor("dscr", [2, 128, 3 * (KA + KB) * 128], BF16, kind="Internal").ap()
        b_dscr = [k.buf("dscr0"), k.buf("dscr1")]

        NT = HALF + NS
        xT = sb(st, "xT", [128, 8, NT], F32)
        hy = sb(st, "hy", [128, 8, NT], BF16)
        b_x = [k.buf("x%d" % i) for i in range(3)]
        b_hy = [k.buf("hy%d" % i) for i in range(5)]

        def hyb(bi):
            return [b_hy[2 * bi], b_hy[2 * bi + 1]] if bi < 2 else [b_hy[4]]

        NRIN = 4
        rfi0 = sb(st, "rfi0p", [128, 8, 2, 128], BF16)
        b_rfi0 = k.buf("rfi0p")
        b_wbb = [k.buf("wbb%d" % i) for i in range(3)]
        ident_f = sb(st, "ident_f", [128, 128], F32)
        ident_b = sb(st, "ident_b", [128, 128], BF16)
        ones_b = sb(st, "ones_b", [128, 128], BF16)
        epsc = sb(st, "epsc", [128, 1], F32)
        b_const = k.buf("const")
        pAB = sb(st, "pAB", [128, 2, 3, 37], F32)
        pC = sb(st, "pC", [128, 2, 2, 2], F32)
        pD = sb(st, "pD", [128, 8, 5], F32)
        WmT = sb(st, "WmT", [128, 2, 4, 128], BF16)
        bsb = sb(st, "bsb", [128, 2, 2, 128], F32)
        w00 = sb(st, "w00", [128, 2, 2], F32)
        b_par = k.buf("par")
        b_pAB, b_pC, b_WmT, b_bsb = k.buf("pAB"), k.buf("pC"), k.buf("WmT"), k.buf("bsb")
        bias2 = sb(st, "bias2", [128, 2, 2, 128], F32)
        b_bias2 = k.buf("bias2")
        negh = sb(st, "negh", [128, 2], F32)
        b_pD = k.buf("pD")
        carry_a = sb(st, "carry_a", [128, 2, 3, KA - 1], BF16)
        carry_b = sb(st, "carry_b", [128, 2, 3, KB - 1], BF16)
        b_carry = k.buf("carry")
        sqr = [sb(st, "sqr%d" % i, [128, 512], BF16) for i in range(3)]
        b_sqr = [k.buf("sqr%d" % i) for i in range(3)]
        nsd = sb(st, "nsd", [128, 512], F32)
        nrs = sb(st, "nrs", [128, 512], F32)
        b_nsd, b_nrs = k.buf("nsd"), k.buf("nrs")

        ps = [st.enter_context(nc.psum_tensor("ps%d" % i, [128, 512], F32)) for i in range(8)]
        b_ps = [k.buf("ps%d" % i) for i in range(8)]
        psn = [0]

        def nextps():
            i = psn[0]
            psn[0] = (i + 1) % 8
            return i

        cnt = {"rin": 0, "ro": 0, "rfi": 0, "rfo": 0, "sqr": 0, "ev": 0}

        def rr(name, n):
            i = cnt[name]
            cnt[name] = (i + 1) % n
            return i

        def act(func, out, in_, reads, writes, scale=1.0, bias=0.0):
            k.op("act", lambda e: e.activation(out=out, in_=in_, func=func, bias=bias, scale=scale), reads, writes)

        def acopy(out, in_, reads, writes):
            k.op("act", lambda e: e.copy(out=out, in_=in_), reads, writes)

        def vcopy(out, in_, reads, writes):
            k.op("dve", lambda e: e.tensor_copy(out=out, in_=in_), reads, writes)

        def evac(out, in_, reads, writes):
            cnt["ev"] += 1
            if cnt["ev"] % 3 == 0:
                vcopy(out, in_, reads, writes)
            else:
                acopy(out, in_, reads, writes)

        def tt(out, a, b, op, reads, writes):
            k.op("dve", lambda e: e.tensor_tensor(out=out, in0=a, in1=b, op=op), reads, writes)

        def stt(out, in0, scalar, in1, op0, op1, reads, writes):
            k.op("dve", lambda e: e.scalar_tensor_tensor(out=out, in0=in0, scalar=scalar, in1=in1, op0=op0, op1=op1),
                 reads, writes)

        def tsc1(out, in0, s1, op0, reads, writes):
            k.op("dve", lambda e: e.tensor_scalar(out=out, in0=in0, scalar1=s1, scalar2=None, op0=op0), reads, writes)

        def tsc(out, in0, s1, s2, op0, op1, reads, writes):
            k.op("dve", lambda e: e.tensor_scalar(out=out, in0=in0, scalar1=s1, scalar2=s2, op0=op0, op1=op1),
                 reads, writes)

        def mm_group(pi, n, pairs, reads):
            np_ = len(pairs)
            fns = []
            for i, (l, r) in enumerate(pairs):
                fns.append(lambda e, l=l, r=r, i=i: e.matmul(ps[pi][:, 0:n], lhsT=l, rhs=r,
                                                           start=(i == 0), stop=(i == np_ - 1)))
            k.group("pe", fns, reads=reads, writes=[b_ps[pi]])

        def transposes(items, reads, writes):
            fns = [(lambda e, o=o, i=i, d=d: e.transpose(out=o, in_=i, identity=d)) for (o, i, d) in items]
            k.group("pe", fns, reads=reads, writes=writes)

        k.op("pool", lambda e: e.memset(ident_f[:], 0.0), writes=[b_const])
        k.op("pool", lambda e: e.affine_select(out=ident_f[:], in_=ident_f[:], pattern=[[-1, 128]],
                                               compare_op=ALU.not_equal, fill=1.0, base=0, channel_multiplier=1),
             reads=[b_const], writes=[b_const])
        k.op("dve", lambda e: e.memset(ones_b[:], 1.0), writes=[b_const])
        k.op("dve", lambda e: e.memset(epsc[:], EPS), writes=[b_const])
        k.op("dve", lambda e: e.memset(negh[:], -0.5), writes=[b_const])
        acopy(ident_b[:], ident_f[:], [b_const], [b_const])

        s0 = ExitStack()
        sD = ExitStack()
        s1h0 = ExitStack()
        pjobs = {}

        def p_alloc(scope, tag, C):
            pjobs[tag] = [sb(scope, "stg" + tag, [40, C], F32), k.buf("stg" + tag), 0, C]

        def p_issue(tag, rows):
            stg, b_stg, R, C = pjobs[tag]
            for ap, r in rows:
                k.dma("sp", stg[R:R + r, 0:C], ap, writes=[b_stg])
                R += r
            pjobs[tag][2] = R

        def p_fin(tag, dst_fn, b_dst=None):
            b_dst = b_dst or b_par
            stg, b_stg, R, C = pjobs[tag]
            for c in range(C // 128):
                pi = nextps()
                transposes([(ps[pi][:, 0:R], stg[0:R, c * 128:(c + 1) * 128], ident_f[0:R, 0:R])],
                           [b_stg, b_const], [b_ps[pi]])
                acopy(dst_fn(c), ps[pi][:, 0:R], [b_ps[pi]], [b_dst])
            k.retire([b_stg])

        for l in range(2):
            p_alloc(s0, "AB%d" % l, CA)
            p_alloc(s0, "C%d" % l, CC)
        wsts = {}
        for l in range(2):
            for h in range(4):
                wsts[(l, h)] = (sb(s0, "wst%d%d" % (l, h), [128, 128], F32), k.buf("wst%d%d" % (l, h)))
        p_alloc(sD, "D", D)
        xin0 = [sb(s1h0, "xin%d" % i, [128, 1024], F32) for i in range(8)]
        b_xin0 = [k.buf("xin%d" % i) for i in range(8)]
        p_issue("D", [(norm_mix_g, 2), (norm_ffn_g, 2), (norm_final_g.unsqueeze(0), 1)])
        for ti in range(8):
            k.dma("sp", xin0[ti][:], xp[ti * 128:(ti + 1) * 128, :], writes=[b_xin0[ti]])
        for l in range(2):
            p_issue("AB%d" % l, [(dw_a[l], KA), (dw_a_bias[l:l + 1, :], 1), (ln_a_g[l:l + 1, :], 1),
                                 (ln_a_b[l:l + 1, :], 1), (conv_b_w[l], KB)])
            p_issue("C%d" % l, [(ln_c_g[l:l + 1, :], 1), (ln_c_b[l:l + 1, :], 1)])
            for h in range(4):
                wst, b_wst = wsts[(l, h)]
                k.dma("sp", wst[:], w_s[l, h], writes=[b_wst])
                q, hh = h // 2, h % 2
                k.dma("sp", bsb[hh * 64:(hh + 1) * 64, l, q, :], b_s[l, h].partition_broadcast(64), writes=[b_bsb])
                k.dma("sp", w00[hh * 64:(hh + 1) * 64, l, q:q + 1],
                      w_s[l, h, 0, 0:1].partition_broadcast(64), writes=[b_bsb])
        def finish_vec(l):
            p_fin("AB%d" % l, lambda c, l=l: pAB[:, l, c, :], b_pAB)
            p_fin("C%d" % l, lambda c, l=l: pC[:, l, c, :], b_pC)

        def finish_wm(l):
            for h in range(4):
                wst, b_wst = wsts[(l, h)]
                k.op("pool", lambda e, wst=wst: e.affine_select(out=wst[:], in_=wst[:], pattern=[[-1, 128]],
                                                                compare_op=ALU.is_ge, fill=0.0, base=0,
                                                                channel_multiplier=1),
                     reads=[b_wst], writes=[b_wst])
                pi = nextps()
                transposes([(ps[pi][:, 0:128], wst[:], ident_f[:])], [b_wst, b_const], [b_ps[pi]])
                acopy(WmT[:, l, h, :], ps[pi][:, 0:128], [b_ps[pi]], [b_WmT])
                k.retire([b_wst])
            for q in range(2):
                pi = nextps()
                fns = [(lambda e, hh=hh, pi=pi: e.matmul(ps[pi][hh * 64:(hh + 1) * 64, 0:128], lhsT=ones_b[:, 0:64],
                                                        rhs=WmT[:, l, 2 * q + hh, :], start=True, stop=True))
                       for hh in range(2)]
                k.group("pe", fns, reads=[b_WmT, b_const], writes=[b_ps[pi]])
                stt(bias2[:, l, q, :], ps[pi][:, 0:128], pC[:, l, q, 1:2], bsb[:, l, q, :], ALU.mult, ALU.add,
                    [b_ps[pi], b_pC, b_bsb], [b_bias2])

        def rmsnorm(col0, n, b_src, gcol, dst_fn, b_dst):
            cols = slice(col0, col0 + n)
            pi = nextps()
            for kk in range(8):
                si = rr("sqr", 3)
                act(AF.Square, sqr[si][:, 0:n], xT[:, kk, cols], [b_src], [b_sqr[si]])
                k.op("pe", lambda e, kk=kk, si=si: e.matmul(ps[pi][:, 0:n], lhsT=ones_b[:], rhs=sqr[si][:, 0:n],
                                                           start=(kk == 0), stop=(kk == 7)),
                     reads=[b_sqr[si], b_const], writes=[b_ps[pi]])
            act(AF.Ln, nsd[:, 0:n], ps[pi][:, 0:n], [b_ps[pi], b_const], [b_nsd], scale=1.0 / D, bias=epsc[:, 0:1])
            act(AF.Exp, nrs[:, 0:n], nsd[:, 0:n], [b_nsd], [b_nrs], scale=-0.5)
            for kk in range(8):
                stt(dst_fn(kk), xT[:, kk, cols], pD[:, kk, gcol:gcol + 1], nrs[:, 0:n], ALU.mult, ALU.mult,
                    [b_src, b_nrs, b_pD], b_dst)

        for half in range(2):
            wblocks = [(0, 512, 0), (512, 512, 1)] + ([(HALF, NS, 2)] if half == 1 else [])
            sblocks = [(i * 256, 256, i, False) for i in range(4)] + ([(HALF, NS, 4, True)] if half == 1 else [])

            with ExitStack() as s1:
                if half == 0:
                    xin, b_xin = xin0, b_xin0
                else:
                    xin, b_xin, xsn, b_xsn = xin1, b_xin1, xsn1, b_xsn1
                for ti in range(8):
                    for hh in range(2):
                        pi = nextps()
                        transposes([(ps[pi][:, c * 128:(c + 1) * 128], xin[ti][:, (4 * hh + c) * 128:(4 * hh + c + 1) * 128],
                                     ident_f[:]) for c in range(4)], [b_xin[ti], b_const], [b_ps[pi]])
                        evac(xT[:, 4 * hh:4 * hh + 4, ti * 128:(ti + 1) * 128],
                             ps[pi][:].rearrange("p (c t) -> p c t", c=4), [b_ps[pi]], [b_x[ti // 4]])
                if half == 1:
                    pi = nextps()
                    transposes([(ps[pi][:, c * NS:(c + 1) * NS], xsn[0:NS, c * 128:(c + 1) * 128],
                                 ident_f[0:NS, 0:NS]) for c in range(8)], [b_xsn, b_const], [b_ps[pi]])
                    acopy(xT[:, :, HALF:HALF + NS], ps[pi][:, 0:8 * NS].rearrange("p (c t) -> p c t", c=8),
                         [b_ps[pi]], [b_x[2]])
                    k.retire([b_xsn])
                k.retire(b_xin)
                if half == 0:
                    p_fin("D", lambda c: pD[:, c, :], b_pD)
                    s1h0.close()
                    sD.close()
                else:
                    s1h1.close()

            for l in range(2):
                with ExitStack() as sm:
                    NTAP = KA + KB
                    diag = sb(sm, "diag", [128, 3, NTAP, 128], BF16)
                    ring_in = [sb(sm, "rin%d" % i, [128, 8, 256], BF16) for i in range(NRIN)]
                    b_rin = [k.buf("rin%d" % i) for i in range(NRIN)]
                    ring_o = [sb(sm, "ro%d" % i, [128, 8, 256], BF16) for i in range(4)]
                    b_ro = [k.buf("ro%d" % i) for i in range(4)]
                    hist_a = sb(sm, "hist_a", [128, 3, KA - 1 + HALF], BF16)
                    hist_b = sb(sm, "hist_b", [128, 3, KB - 1 + HALF], BF16)
                    wbb = sb(sm, "wbb", [128, 8, 384], BF16)
                    u_t = sb(sm, "u_t", [128, 2, NT], BF16)
                    gv = sb(sm, "gv", [128, 2, NT], F32)
                    tail_a = sb(sm, "tail_a", [128, 3, KA - 1], F32)
                    tail_b = sb(sm, "tail_b", [128, 3, KB - 1], F32)
                    osb = sb(sm, "osb", [32, 384], F32)
                    names = ["diag0", "diag1", "diag2", "hist_a", "hist_b", "u", "gv", "tail", "osb"]
                    B = {n_: k.buf(n_) for n_ in names}
                    if half == 1:
                        has = sb(sm, "has", [128, 3, NS, KA], F32)
                        hbs = sb(sm, "hbs", [128, 3, NS, KB], F32)
                        stg2 = [sb(sm, "stg2_%d" % i, [120, 384], F32) for i in range(4)]
                        stgb = sb(sm, "stgb", [32, 384], F32)
                        cvs = sb(sm, "cvs", [128, 3, NS], F32)
                        cvb = sb(sm, "cvb", [128, 3, NS], F32)
                        prod = sb(sm, "prod", [128, NS, KA], F32)
                        for n_ in ["has", "hbs", "stg20", "stg21", "stg22", "stg23", "stgb", "cvs", "cvb", "prod"]:
                            B[n_] = k.buf(n_)
                    mix_bufs = list(B.values())
                    for b_ in b_wbb:
                        for s_, v_ in k.pending.items():
                            _merge(b_.w, s_, v_)

                    cnt2 = {"sig": 0, "bcs": 0, "g": 0, "t": 0, "cbs": 0, "dg": 0}

                    def r2(name):
                        i = cnt2[name]
                        cnt2[name] = (i + 1) % 2
                        return i

                    if half == 0:
                        k.op("dve", lambda e: e.memset(hist_a[:, :, 0:KA - 1], 0.0), writes=[B["hist_a"]])
                        k.op("dve", lambda e: e.memset(hist_b[:, :, 0:KB - 1], 0.0), writes=[B["hist_b"]])
                    else:
                        vcopy(hist_a[:, :, 0:KA - 1], carry_a[:, l], [b_carry], [B["hist_a"]])
                        vcopy(hist_b[:, :, 0:KB - 1], carry_b[:, l], [b_carry], [B["hist_b"]])
                        k.dma("sp", diag[:].rearrange("p j t m -> p (j t m)"), dscr[l], reads=[b_dscr[l]], writes=[B["diag0"], B["diag1"], B["diag2"]])

                    if half == 1:
                        for i in range(4):
                            k.dma("pool", stg2[i][:], sca[l, 4 * i:4 * i + 4].rearrange("n k c -> (n k) c"),
                                  writes=[B["stg2%d" % i]])
                        k.dma("pool", stgb[:], scb[l].rearrange("n k c -> (n k) c"), writes=[B["stgb"]])
                    for (c0, n, bi) in wblocks:
                        rmsnorm(c0, n, b_x[bi], l, lambda kk, c0=c0, n=n: hy[:, kk, c0:c0 + n], hyb(bi))
                    if half == 0 and l == 0:
                        finish_vec(0)

                    if half == 1:
                        for i in range(4):
                            si = i
                            for j in range(3):
                                pi = nextps()
                                transposes([(ps[pi][:, 0:120], stg2[si][0:120, j * 128:(j + 1) * 128],
                                             ident_f[0:120, 0:120])], [B["stg2%d" % si], b_const], [b_ps[pi]])
                                acopy(has[:, j, 4 * i:4 * i + 4, 0:KA - 1],
                                     ps[pi][:, 0:120].rearrange("p (n k) -> p n k", n=4), [b_ps[pi]], [B["has"]])
                        for j in range(3):
                            pi = nextps()
                            transposes([(ps[pi][:, 0:32], stgb[0:32, j * 128:(j + 1) * 128], ident_f[0:32, 0:32])],
                                       [B["stgb"], b_const], [b_ps[pi]])
                            acopy(hbs[:, j, :, 0:KB - 1], ps[pi][:, 0:32].rearrange("p (n k) -> p n k", n=NS),
                                 [b_ps[pi]], [B["hbs"]])
                        k.dma("sp", ncas[l, :, 0:KA - 2, :], sca[l, :, 1:KA - 1, :])
                        k.dma("sp", ncbs[l, :, 0:KB - 2, :], scb[l, :, 1:KB - 1, :])

                    diag_jobs = []
                    if half == 0:
                        for j in range(3):
                            diag_jobs.append((j, 0, KA, 0))
                            diag_jobs.append((j, KA, KB, 34))

                    def build_diag(njobs):
                        for _ in range(njobs):
                            if not diag_jobs:
                                return
                            j, t0, nt, col = diag_jobs.pop(0)
                            tt(diag[:, j, t0:t0 + nt, :], ident_b[:].unsqueeze(1).broadcast_to([128, nt, 128]),
                               pAB[:, l, j, col:col + nt].unsqueeze(2).broadcast_to([128, nt, 128]), ALU.mult,
                               [b_const, b_pAB], [B["diag%d" % j]])

                    sw = ExitStack()
                    sig = [sb(sw, "sig%d" % i, [128, 512], F32) for i in range(4)]
                    bcs = [sb(sw, "bcs%d" % i, [128, 512], F32) for i in range(4)]
                    sig += [sb(sw, "sig_s%d" % i, [128, NS], F32) for i in range(2)]
                    bcs += [sb(sw, "bcs_s%d" % i, [128, NS], F32) for i in range(2)]
                    g1 = [sb(sw, "g1_%d" % i, [128, 512], F32) for i in range(2)]
                    g2 = [sb(sw, "g2_%d" % i, [128, 512], F32) for i in range(2)]
                    wnames = ["sig%d" % i for i in range(6)] + ["bcs%d" % i for i in range(6)] + ["g10", "g11", "g20", "g21"]
                    for n_ in wnames:
                        B[n_] = k.buf(n_)
                    units = [(4, 5), (2, 3), (0, 1), (12, 13), (6, 7), (14, 15), (8,), (16, 17), (18,)]
                    sig_of = {}
                    bcs_of = {}
                    sig_live = [set(), set(), set()]
                    bcs_live = [set(), set(), set()]

                    def take(live, bi):
                        base = 2 * bi if bi < 2 else 4
                        for sl in (base, base + 1):
                            if sl not in live[bi]:
                                live[bi].add(sl)
                                return sl
                        raise RuntimeError("no free gate buffer")
                    for un in units:
                        work = []
                        ri = rr("rin", NRIN)
                        nc_ = len(un) * 128
                        k.dma("pool", ring_in[ri][:, :, 0:nc_],
                              w_in[l][:, un[0] * 128:un[0] * 128 + nc_].rearrange("(k p) c -> p k c", p=128),
                              writes=[b_rin[ri]])
                        cs = sorted(un, key=lambda c: (0 if (c <= 2 or 6 <= c <= 8) else 1, c))
                        for c in cs:
                            work.append((c, ri, (c - un[0]) * 128))
                        if un == units[0]:
                            work0 = work
                            continue
                        if un == units[1]:
                            it = [(w_, wb) for wb in wblocks for w_ in (work0 + work)]
                        else:
                            it = [(w_, wb) for w_ in work for wb in wblocks]
                        for ((c, ri, coff), (c0, n, bi)) in it:
                            if True:
                                cols = slice(c0, c0 + n)
                                pi = nextps()
                                mm_group(pi, n, [(ring_in[ri][:, kk, coff:coff + 128], hy[:, kk, cols]) for kk in range(8)],
                                         [b_rin[ri]] + hyb(bi))
                                P = ps[pi][:, 0:n]
                                bp = b_ps[pi]
                                if 3 <= c <= 5:
                                    si = take(sig_live, bi)
                                    sig_of[(c - 3, bi)] = si
                                    act(AF.Sigmoid, sig[si][:, 0:n], P, [bp], [B["sig%d" % si]])
                                elif c <= 2:
                                    j = c
                                    si = sig_of[(j, bi)]
                                    sig_live[bi].discard(si)
                                    if bi < 2:
                                        tt(hist_a[:, j, KA - 1 + c0:KA - 1 + c0 + n], P, sig[si][:, 0:n], ALU.mult,
                                           [bp, B["sig%d" % si]], [B["hist_a"]])
                                        if half == 1 and bi == 1:
                                            tt(tail_a[:, j, :], ps[pi][:, n - (KA - 1):n], sig[si][:, n - (KA - 1):n],
                                               ALU.mult, [bp, B["sig%d" % si]], [B["tail"]])
                                    else:
                                        tt(has[:, j, :, KA - 1], P, sig[si][:, 0:n], ALU.mult,
                                           [bp, B["sig%d" % si]], [B["has"]])
                                elif 12 <= c <= 14:
                                    si = take(bcs_live, bi)
                                    bcs_of[(c - 12, bi)] = si
                                    acopy(bcs[si][:, 0:n], P, [bp], [B["bcs%d" % si]])
                                elif 6 <= c <= 8:
                                    j = c - 6
                                    si = bcs_of[(j, bi)]
                                    bcs_live[bi].discard(si)
                                    if bi < 2:
                                        tt(hist_b[:, j, KB - 1 + c0:KB - 1 + c0 + n], P, bcs[si][:, 0:n], ALU.mult,
                                           [bp, B["bcs%d" % si]], [B["hist_b"]])
                                        if half == 1 and bi == 1:
                                            tt(tail_b[:, j, :], ps[pi][:, n - (KB - 1):n], bcs[si][:, n - (KB - 1):n],
                                               ALU.mult, [bp, B["bcs%d" % si]], [B["tail"]])
                                    else:
                                        tt(hbs[:, j, :, KB - 1], P, bcs[si][:, 0:n], ALU.mult,
                                           [bp, B["bcs%d" % si]], [B["hbs"]])
                                else:
                                    gi = r2("g")
                                    G1, G2 = g1[gi][:, 0:n], g2[gi][:, 0:n]
                                    bg1, bg2 = B["g1%d" % gi], B["g2%d" % gi]
                                    if c <= 16:
                                        act(AF.Gelu_apprx_tanh, u_t[:, c - 15, cols], P, [bp], [B["u"]])
                                    else:
                                        act(AF.Gelu_apprx_tanh, gv[:, c - 17, cols], P, [bp], [B["gv"]])
                            build_diag(1)
                    build_diag(1000)
                    if half == 0:
                        k.dma("sp", dscr[l], diag[:].rearrange("p j t m -> p (j t m)"), reads=[B["diag0"], B["diag1"], B["diag2"]], writes=[b_dscr[l]])
                    k.dma("pool", wbb[:], w_in[l][:, 9 * 128:12 * 128].rearrange("(k p) c -> p k c", p=128),
                          writes=b_wbb)

                    k.retire([B[n_] for n_ in wnames])
                    sw.close()
                    for g_ in range(2):
                        k.dma("pool", rfi0[:, :, g_, :],
                              w_ffn_in[l][:, g_ * DFF:g_ * DFF + 128].rearrange("(k p) c -> p k c", p=128), writes=[b_rfi0])
                    if half == 0 and l == 0:
                        finish_wm(0)
                    for m2 in range(4):
                        k.dma("pool", ring_o[m2][:], w_o[l][:, m2 * 256:(m2 + 1) * 256].rearrange("(k p) c -> p k c", p=128),
                              writes=[b_ro[m2]])

                    def wo_emit(wb, ms):
                        c0, n, bi = wb
                        cols = slice(c0, c0 + n)
                        for m in ms:
                            m2, mm_ = m // 2, m % 2
                            pi = nextps()
                            mm_group(pi, n, [(ring_o[m2][:, kk, mm_ * 128:(mm_ + 1) * 128], hy[:, kk, cols])
                                             for kk in range(8)], [b_ro[m2]] + hyb(bi))
                            tt(xT[:, m, cols], ps[pi][:, 0:n], xT[:, m, cols], ALU.add, [b_ps[pi], b_x[bi]], [b_x[bi]])

                    v32 = [sb(sm, "v32_%d" % i, [128, 3, 256], F32) for i in range(2)]
                    vbf = sb(sm, "vbf", [128, 5, 256], BF16)
                    sqbf = sb(sm, "sqbf", [128, 5, 256], BF16)
                    mean = sb(sm, "mean", [128, 2, 256], F32)
                    mrC = sb(sm, "mrC", [128, 2, NS], F32)
                    rstd = sb(sm, "rstd", [128, 2, 256], F32)
                    tA = [sb(sm, "tA%d" % i, [128, 256], F32) for i in range(3)]
                    tS = [sb(sm, "tS%d" % i, [128, 256], F32) for i in range(3)]
                    tN = [sb(sm, "tN%d" % i, [128, 256], F32) for i in range(3)]
                    tT = [sb(sm, "tT%d" % i, [128, 256], F32) for i in range(2)]
                    cbs = [sb(sm, "cbs%d" % i, [128, 256], F32) for i in range(2)]
                    vn32 = sb(sm, "vn32", [128, 2, 256], F32)
                    vntok2 = [sb(sm, "vntok%d" % i, [128, 2, 256], BF16) for i in range(2)]
                    vntok = vntok2[0]
                    tC = sb(sm, "tC", [128, 256], F32)
                    bst2 = [sb(sm, "bst%d" % i, [128, 2, 6], F32) for i in range(2)]
                    mv2 = [sb(sm, "mv%d" % i, [128, 2, 2], F32) for i in range(2)]
                    rs22 = [sb(sm, "rs2%d" % i, [128, 4], F32) for i in range(2)]
                    for n_ in ["bst0", "bst1", "mv0", "mv1", "rs20", "rs21", "vntokB"]:
                        B[n_] = k.buf(n_)
                    mix_bufs = mix_bufs + [B[n_] for n_ in ["bst0", "bst1", "mv0", "mv1", "rs20", "rs21", "vntokB"]]
                    pnames = ["v320", "v321", "vbf", "sqbf", "mean0", "mean1", "mean2", "rstd0", "rstd1", "rstd2", "tA0", "tA1", "tA2", "tS0", "tS1", "tS2",
                              "tN0", "tN1", "tN2", "tT0", "tT1", "cbs0", "cbs1", "vn32", "vntok", "tC", "ybt0", "ybt1"]
                    for n_ in pnames:
                        B[n_] = k.buf(n_)
                    mix_bufs = mix_bufs + [B[n_] for n_ in pnames]
                    S = {}

                    def st_CA(s):
                        c0, n, sbi, sample = sblocks[s]
                        conv = []
                        if not sample:
                            for j in range(3):
                                pi = nextps()
                                mm_group(pi, n, [(diag[:, j, kk, :], hist_a[:, j, c0 + kk:c0 + kk + n]) for kk in range(KA)],
                                         [B["diag%d" % j], B["hist_a"]])
                                conv.append((ps[pi][:, 0:n], b_ps[pi]))
                        else:
                            for j in range(3):
                                tt(prod[:], has[:, j], pAB[:, l, j, 0:KA].unsqueeze(1).broadcast_to([128, NS, KA]),
                                   ALU.mult, [B["has"], b_pAB], [B["prod"]])
                                k.op("dve", lambda e, j=j: e.tensor_reduce(out=cvs[:, j, :], in_=prod[:], axis=AX.X,
                                                                          op=ALU.add), [B["prod"]], [B["cvs"]])
                                conv.append((cvs[:, j, :], B["cvs"]))
                        S[s] = conv

                    def st_EV(s):
                        c0, n, sbi, sample = sblocks[s]
                        cols = slice(c0, c0 + n)
                        for j in range(3):
                            src, bsrc = S[s][j]
                            bias = pAB[:, l, j, 31:32]
                            act(AF.Identity, v32[s % 2][:, j, 0:n], src, [bsrc, b_pAB], [B["v32%d" % (s % 2)]], bias=bias)
                            act(AF.Identity, vbf[:, j, 0:n], src, [bsrc, b_pAB], [B["vbf"]], bias=bias)
                            act(AF.Square, sqbf[:, j, 0:n], src, [bsrc, b_pAB], [B["sqbf"]], bias=bias)
                        for q in range(2 if sample else 0):
                            act(AF.Identity, vbf[:, 3 + q, 0:n], gv[:, q, cols], [B["gv"]], [B["vbf"]])
                            act(AF.Square, sqbf[:, 3 + q, 0:n], gv[:, q, cols], [B["gv"]], [B["sqbf"]])

                    def ln_stats(n, nch, off, invn, mslot):
                        pi = nextps()
                        mm_group(pi, n, [(ones_b[:], vbf[:, off + j, 0:n]) for j in range(nch)], [B["vbf"], b_const])
                        fns = []
                        for j in range(nch):
                            fns.append(lambda e, j=j: e.matmul(ps[pi][:, 256:256 + n], lhsT=ones_b[:],
                                                              rhs=sqbf[:, off + j, 0:n], start=(j == 0),
                                                              stop=(j == nch - 1)))
                        k.group("pe", fns, reads=[B["sqbf"], b_const], writes=[b_ps[pi]])
                        ti = r2("t")
                        T = tT[ti][:, 0:n]
                        bT = B["tT%d" % ti]
                        M_ap = mean[:, mslot, 0:n] if mslot < 2 else mrC[:, 0, 0:n]
                        R_ap = rstd[:, mslot, 0:n] if mslot < 2 else mrC[:, 1, 0:n]
                        act(AF.Identity, M_ap, ps[pi][:, 0:n], [b_ps[pi]], [B["mean%d" % mslot]], scale=invn)
                        act(AF.Square, T, ps[pi][:, 0:n], [b_ps[pi]], [bT], scale=invn)
                        stt(T, ps[pi][:, 256:256 + n], invn, T, ALU.mult, ALU.subtract, [b_ps[pi], bT], [bT])
                        act(AF.Ln, T, T, [bT, b_const], [bT], bias=epsc[:, 0:1])
                        act(AF.Exp, R_ap, T, [bT], [B["rstd%d" % mslot]], scale=-0.5)

                    def st_ST(s):
                        c0, n, sbi, sample = sblocks[s]
                        if sample:
                            ln_stats(n, 2, 3, 1.0 / CC, 2)
                        ln_stats(n, 3, 0, 1.0 / CA, s % 2)

                    def st_BB(s):
                        c0, n, sbi, sample = sblocks[s]
                        cols = slice(c0, c0 + n)
                        pbs = []
                        for j in range(3):
                            pb = nextps()
                            mm_group(pb, n, [(wbb[:, kk, j * 128:(j + 1) * 128], hy[:, kk, cols]) for kk in range(8)],
                                     [b_wbb[j], b_hy[sbi]])
                            pbs.append(pb)
                        for j in range(3):
                            ci = r2("cbs")
                            if not sample:
                                pi = nextps()
                                mm_group(pi, n, [(diag[:, j, KA + kk, :], hist_b[:, j, c0 + kk:c0 + kk + n])
                                                 for kk in range(KB)], [B["diag%d" % j], B["hist_b"]])
                                acopy(cbs[ci][:, 0:n], ps[pi][:, 0:n], [b_ps[pi]], [B["cbs%d" % ci]])
                                cb_ap, b_cb = cbs[ci][:, 0:n], B["cbs%d" % ci]
                            else:
                                tt(prod[:, :, 0:KB], hbs[:, j], pAB[:, l, j, 34:37].unsqueeze(1).broadcast_to([128, NS, KB]),
                                   ALU.mult, [B["hbs"], b_pAB], [B["prod"]])
                                k.op("dve", lambda e, j=j: e.tensor_reduce(out=cvb[:, j, :], in_=prod[:, :, 0:KB],
                                                                          axis=AX.X, op=ALU.add), [B["prod"]], [B["cvb"]])
                                cb_ap, b_cb = cvb[:, j, :], B["cvb"]
                            tt(hy[:, 3 + j, cols], ps[pbs[j]][:, 0:n], cb_ap, ALU.mult, [b_ps[pbs[j]], b_cb], [b_hy[sbi]])

                    def st_LN(s):
                        c0, n, sbi, sample = sblocks[s]
                        cols = slice(c0, c0 + n)
                        for q in range(2):
                            ti = r2("t")
                            TA = tA[ti][:, 0:n]
                            bA = B["tA%d" % ti]
                            tt(TA, gv[:, q, cols], mrC[:, 0, 0:n], ALU.subtract, [B["gv"], B["mean2"]], [bA])
                            tt(TA, TA, mrC[:, 1, 0:n], ALU.mult, [bA, B["rstd2"]], [bA])
                            act(AF.Identity, vn32[:, q, 0:n], TA, [bA, b_pC], [B["vn32"]],
                                scale=pC[:, l, q, 0:1], bias=pC[:, l, q, 1:2])

                    def st_LNA(s):
                        c0, n, sbi, sample = sblocks[s]
                        cols = slice(c0, c0 + n)
                        V = v32[s % 2]
                        bV = B["v32%d" % (s % 2)]
                        for j in range(3):
                            TA, bA = tA[j][:, 0:n], B["tA%d" % j]
                            tt(TA, V[:, j, 0:n], mean[:, s % 2, 0:n], ALU.subtract, [bV, B["mean%d" % (s % 2)]], [bA])
                            tt(TA, TA, rstd[:, s % 2, 0:n], ALU.mult, [bA, B["rstd%d" % (s % 2)]], [bA])
                        for j in range(3):
                            TA, bA = tA[j][:, 0:n], B["tA%d" % j]
                            act(AF.Silu, hy[:, j, cols], TA, [bA, b_pAB], [b_hy[sbi]],
                                scale=pAB[:, l, j, 32:33], bias=pAB[:, l, j, 33:34])

                    def st_C1(s):
                        c0, n, sbi, sample = sblocks[s]
                        z = s % 2
                        bst, mv, rs2, VT = bst2[z], mv2[z], rs22[z], vntok2[z]
                        bB, bM, bR, bV = B["bst%d" % z], B["mv%d" % z], B["rs2%d" % z], (B["vntok"] if z == 0 else B["vntokB"])
                        pi = nextps()
                        transposes([(ps[pi][:, ti_ * 256 + q * 128: ti_ * 256 + (q + 1) * 128],
                                     gv[:, q, c0 + ti_ * 128:c0 + (ti_ + 1) * 128], ident_f[:])
                                    for ti_ in range(2) for q in range(2)], [B["gv"], b_const], [b_ps[pi]])
                        for ti_ in range(2):
                            k.op("dve", lambda e, ti_=ti_: e.bn_stats(out=bst[:, ti_, :], in_=ps[pi][:, ti_ * 256:(ti_ + 1) * 256]),
                                 [b_ps[pi]], [bB])
                            k.op("dve", lambda e, ti_=ti_: e.bn_aggr(out=mv[:, ti_, :], in_=bst[:, ti_, :]), [bB], [bM])
                        k.op("pool", lambda e: e.tensor_scalar(out=rs2[:, 0:2], in0=mv[:, :, 1], scalar1=1.0, scalar2=EPS,
                                                               op0=ALU.mult, op1=ALU.add), [bM], [bR])
                        k.op("pool", lambda e: e.tensor_tensor(out=rs2[:, 2:4], in0=rs2[:, 0:2], in1=negh[:], op=ALU.pow),
                             [bR, b_const], [bR])
                        for ti_ in range(2):
                            tsc(VT[:, ti_, :], ps[pi][:, ti_ * 256:(ti_ + 1) * 256], mv[:, ti_, 0:1],
                                rs2[:, 2 + ti_:3 + ti_], ALU.subtract, ALU.mult, [b_ps[pi], bM, bR], [bV])

                    def st_C2(s):
                        c0, n, sbi, sample = sblocks[s]
                        cols = slice(c0, c0 + n)
                        z = s % 2
                        VT = vntok2[z]
                        bV = B["vntok"] if z == 0 else B["vntokB"]
                        for q in range(2):
                            pi = nextps()
                            fns = []
                            for ti_ in range(2):
                                for hh in range(2):
                                    h = 2 * q + hh
                                    fns.append(lambda e, ti_=ti_, hh=hh, h=h, pi=pi: e.matmul(
                                        ps[pi][hh * 64:(hh + 1) * 64, ti_ * 128:(ti_ + 1) * 128],
                                        lhsT=VT[:, ti_, h * 64:(h + 1) * 64], rhs=WmT[:, l, h, :],
                                        start=True, stop=True))
                            k.group("pe", fns, reads=[bV, b_WmT], writes=[b_ps[pi]])
                            stt(tC[:, 0:256].rearrange("p (t s) -> p t s", t=2),
                                ps[pi][:, 0:256].rearrange("p (t s) -> p t s", t=2), pC[:, l, q, 0:1],
                                bias2[:, l, q, :].unsqueeze(1).broadcast_to([128, 2, 128]), ALU.mult, ALU.add,
                                [b_ps[pi], b_pC, b_bias2], [B["tC"]])
                            tt(hy[:, 6 + q, cols], tC[:, 0:256], u_t[:, q, cols], ALU.mult, [B["tC"], B["u"]],
                               [b_hy[sbi]])

                    def st_TR(s):
                        c0, n, sbi, sample = sblocks[s]
                        if not sample:
                            pi = nextps()
                            transposes([(ps[pi][:, ti_ * 256 + q * 128: ti_ * 256 + (q + 1) * 128],
                                         vn32[:, q, ti_ * 128:(ti_ + 1) * 128], ident_f[:])
                                        for ti_ in range(2) for q in range(2)], [B["vn32"], b_const], [b_ps[pi]])
                            acopy(vntok[:].rearrange("p t c -> p (t c)"), ps[pi][:, 0:512], [b_ps[pi]], [B["vntok"]])
                        else:
                            pi = nextps()
                            transposes([(ps[pi][0:NS, q * 128:(q + 1) * 128], vn32[:, q, 0:NS], ident_f[:])
                                        for q in range(2)], [B["vn32"], b_const], [b_ps[pi]])
                            acopy(osb[0:NS, 0:256], ps[pi][0:NS, 0:256], [b_ps[pi]], [B["osb"]])
                            k.dma("sp", ncv[l], osb[0:NS, 0:256], reads=[B["osb"]])

                    def st_SP(s):
                        c0, n, sbi, sample = sblocks[s]
                        cols = slice(c0, c0 + n)
                        if not sample:
                            for q in range(2):
                                pi = nextps()
                                fns = []
                                for ti_ in range(2):
                                    for hh in range(2):
                                        h = 2 * q + hh
                                        fns.append(lambda e, ti_=ti_, hh=hh, h=h, pi=pi: e.matmul(
                                            ps[pi][hh * 64:(hh + 1) * 64, ti_ * 128:(ti_ + 1) * 128],
                                            lhsT=vntok[:, ti_, h * 64:(h + 1) * 64], rhs=WmT[:, l, h, :],
                                            start=True, stop=True))
                                k.group("pe", fns, reads=[B["vntok"], b_WmT], writes=[b_ps[pi]])
                                tt(tC[:, 0:256].rearrange("p (t s) -> p t s", t=2),
                                   ps[pi][:, 0:256].rearrange("p (t s) -> p t s", t=2),
                                   bsb[:, l, q, :].unsqueeze(1).broadcast_to([128, 2, 128]), ALU.add,
                                   [b_ps[pi], b_bsb], [B["tC"]])
                                tt(hy[:, 6 + q, cols], tC[:, 0:256], u_t[:, q, cols], ALU.mult, [B["tC"], B["u"]],
                                   [b_hy[sbi]])
                        else:
                            for q in range(2):
                                tsc(tC[:, 0:n], vn32[:, q, 0:n], w00[:, l, q:q + 1], bsb[:, l, q, 0:1], ALU.mult, ALU.add,
                                    [B["vn32"], b_bsb], [B["tC"]])
                                tt(hy[:, 6 + q, cols], tC[:, 0:n], u_t[:, q, cols], ALU.mult, [B["tC"], B["u"]],
                                   [b_hy[sbi]])

                    NSB = len(sblocks)
                    wo_done = set()
                    st_CA(0)
                    st_C1(0)
                    st_EV(0); st_BB(0); st_ST(0)
                    for s in range(NSB):
                        nx = s + 1 < NSB
                        if nx:
                            st_CA(s + 1)
                        is_s = sblocks[s][3]
                        if nx and not sblocks[s + 1][3]:
                            st_C1(s + 1)
                        if is_s:
                            wo_emit(wblocks[0], range(0, 8))
                            wo_emit(wblocks[1], range(0, 3))
                            st_ST(s)
                            wo_emit(wblocks[1], range(3, 8))
                            wo_done.add(1)
                            st_LN(s)
                            st_TR(s)
                        if nx:
                            st_EV(s + 1)
                            st_BB(s + 1)
                        if nx and not sblocks[s + 1][3]:
                            st_ST(s + 1)
                        st_LNA(s)
                        if not is_s:
                            st_C2(s)
                        if s == 3 and half == 0:
                            wo_emit(wblocks[0], range(0, 8))
                        if is_s:
                            st_SP(s)

                    if half == 1:
                        pi = nextps()
                        transposes([(ps[pi][0:KA - 1, j * 128:(j + 1) * 128], tail_a[:, j, :], ident_f[:]) for j in range(3)],
                                   [B["tail"], b_const], [b_ps[pi]])
                        acopy(osb[0:KA - 1, :], ps[pi][0:KA - 1, 0:384], [b_ps[pi]], [B["osb"]])
                        k.dma("sp", ncap[l], osb[0:KA - 1, :], reads=[B["osb"]])
                        pi = nextps()
                        transposes([(ps[pi][0:KB - 1, j * 128:(j + 1) * 128], tail_b[:, j, :], ident_f[:]) for j in range(3)],
                                   [B["tail"], b_const], [b_ps[pi]])
                        acopy(osb[0:KB - 1, :], ps[pi][0:KB - 1, 0:384], [b_ps[pi]], [B["osb"]])
                        k.dma("sp", ncbp[l], osb[0:KB - 1, :], reads=[B["osb"]])
                        pi = nextps()
                        transposes([(ps[pi][0:NS, j * 128:(j + 1) * 128], has[:, j, :, KA - 1], ident_f[:]) for j in range(3)],
                                   [B["has"], b_const], [b_ps[pi]])
                        acopy(osb[0:NS, :], ps[pi][0:NS, 0:384], [b_ps[pi]], [B["osb"]])
                        k.dma("sp", ncas[l, :, KA - 2, :], osb[0:NS, :], reads=[B["osb"]])
                        pi = nextps()
                        transposes([(ps[pi][0:NS, j * 128:(j + 1) * 128], hbs[:, j, :, KB - 1], ident_f[:]) for j in range(3)],
                                   [B["hbs"], b_const], [b_ps[pi]])
                        acopy(osb[0:NS, :], ps[pi][0:NS, 0:384], [b_ps[pi]], [B["osb"]])
                        k.dma("sp", ncbs[l, :, KB - 2, :], osb[0:NS, :], reads=[B["osb"]])
                    else:
                        acopy(carry_a[:, l], hist_a[:, :, HALF:HALF + KA - 1], [B["hist_a"]], [b_carry])
                        acopy(carry_b[:, l], hist_b[:, :, HALF:HALF + KB - 1], [B["hist_b"]], [b_carry])

                    if half == 0 and l == 0:
                        finish_vec(1)
                        finish_wm(1)
                    for (c0, n, bi) in wblocks[1:]:
                        if bi not in wo_done:
                            wo_emit((c0, n, bi), range(8))
                    k.retire(mix_bufs + b_rin + b_ro)
                    k.retire(b_wbb, reads_only=True)
                if half == 0 and l == 0:
                    s0.close()

                with ExitStack() as sf:
                    gT = sb(sf, "gT", [128, NJ, NT], BF16)
                    fs = [sb(sf, "fs%d" % i, [128, 512], F32) for i in range(2)]
                    b_gT = k.buf("gT")
                    b_fs = [k.buf("fs%d" % i) for i in range(2)]
                    ring_fi = [sb(sf, "rfi%d" % i, [128, 8, 2, 512], BF16) for i in range(3)]
                    b_rfi = [k.buf("rfi%d" % i) for i in range(3)]
                    ring_fo = [sb(sf, "rfo%d" % i, [128, NJ, 256], BF16) for i in range(2)]
                    b_rfo = [k.buf("rfo%d" % i) for i in range(2)]
                    for (c0, n, bi) in wblocks:
                        rmsnorm(c0, n, b_x[bi], 2 + l, lambda kk, c0=c0, n=n: hy[:, kk, c0:c0 + n], hyb(bi))
                    fsi = 0
                    for j4 in [0, 1, 5, 9, 13, 17, 21]:
                        if j4 == 0:
                            nj = 1
                            RF, bRF = rfi0, b_rfi0
                        else:
                            nj = min(4, NJ - j4)
                            ri = rr("rfi", 3)
                            RF, bRF = ring_fi[ri], b_rfi[ri]
                            for g_ in range(2):
                                k.dma("pool", RF[:, :, g_, 0:nj * 128],
                                      w_ffn_in[l][:, g_ * DFF + j4 * 128:g_ * DFF + (j4 + nj) * 128].rearrange("(k p) c -> p k c", p=128),
                                      writes=[bRF])
                        if j4 == 0:
                            first = (RF, bRF)
                            continue
                        if j4 == 1:
                            it = [(j_, jj_, R_, wb) for wb in wblocks
                                  for (j_, jj_, R_) in [(0, 0, first)] + [(j4 + q_, q_, (RF, bRF)) for q_ in range(nj)]]
                        else:
                            it = [(j4 + jj, jj, (RF, bRF), wb) for jj in range(nj) for wb in wblocks]
                        for (j, jj, (RFx, bRFx), (c0, n, bi)) in it:
                            cols = slice(c0, c0 + n)
                            pg = nextps()
                            mm_group(pg, n, [(RFx[:, kk, 0, jj * 128:(jj + 1) * 128], hy[:, kk, cols])
                                             for kk in range(8)], [bRFx] + hyb(bi))
                            pu = nextps()
                            mm_group(pu, n, [(RFx[:, kk, 1, jj * 128:(jj + 1) * 128], hy[:, kk, cols])
                                             for kk in range(8)], [bRFx] + hyb(bi))
                            fsi ^= 1
                            FS = fs[fsi][:, 0:n]
                            act(AF.Silu, FS, ps[pg][:, 0:n], [b_ps[pg]], [b_fs[fsi]])
                            tt(gT[:, j, cols], ps[pu][:, 0:n], FS, ALU.mult, [b_ps[pu], b_fs[fsi]], [b_gT])
                    for m2 in range(4):
                        ri = rr("rfo", 2)
                        k.dma("pool", ring_fo[ri][:],
                              w_ffn_out[l][:, m2 * 256:(m2 + 1) * 256].rearrange("(j p) c -> p j c", p=128),
                              writes=[b_rfo[ri]])
                        if m2 == 3:
                            it = [(mm_, wb) for wb in wblocks for mm_ in range(2)]
                        else:
                            it = [(mm_, wb) for mm_ in range(2) for wb in wblocks]
                        for (mm_, (c0, n, bi)) in it:
                            m = 2 * m2 + mm_
                            cols = slice(c0, c0 + n)
                            pi = nextps()
                            mm_group(pi, n, [(ring_fo[ri][:, j, mm_ * 128:(mm_ + 1) * 128], gT[:, j, cols])
                                             for j in range(NJ)], [b_rfo[ri], b_gT])
                            tt(xT[:, m, cols], ps[pi][:, 0:n], xT[:, m, cols], ALU.add, [b_ps[pi], b_x[bi]], [b_x[bi]])
                    k.retire([b_gT] + b_fs + b_rfi + b_rfo)

            if half == 0:
                s1h1 = ExitStack()
                gfb0 = sb(s1h1, "gfb0", [128, 1024], F32)
                b_gfb0 = k.buf("gfb0")
                k.dma("sp", gfb0[:], norm_final_g.partition_broadcast(128), writes=[b_gfb0])
                xin1 = [sb(s1h1, "xin%d" % i, [128, 1024], F32) for i in range(8)]
                b_xin1 = [k.buf("xin%d" % i) for i in range(8)]
                for ti in range(8):
                    r0 = HALF + ti * 128
                    k.dma("sp", xin1[ti][:], xp[r0:r0 + 128, :], writes=[b_xin1[ti]])
                xsn1 = sb(s1h1, "xsn", [NS, 1024], F32)
                b_xsn1 = k.buf("xsn")
                k.dma("sp", xsn1[:], xs, writes=[b_xsn1])

            with ExitStack() as s2:
                ost = [sb(s2, "ost%d" % i, [128, 1024], F32) for i in range(3)]
                b_ost = [k.buf("ost%d" % i) for i in range(3)]
                junk = sb(s2, "junk", [128, 512], BF16)
                b_junk = k.buf("junk")
                ssq = [sb(s2, "ssq%d" % i, [128, 8], F32) for i in range(3)]
                b_ssq = [k.buf("ssq%d" % i) for i in range(3)]
                if half == 0:
                    gfb, b_gfb = gfb0, b_gfb0
                else:
                    gfb = sb(s2, "gfb", [128, 1024], F32)
                    b_gfb = k.buf("gfb")
                    k.dma("sp", gfb[:], norm_final_g.partition_broadcast(128), writes=[b_gfb])
                tiles = [(ti * 128, 128, ti // 4) for ti in range(8)] + ([(HALF, NS, 2)] if half == 1 else [])
                groups = [tiles[i:i + 2] for i in range(0, 8, 2)] + ([[tiles[8]]] if half == 1 else [])
                oidx = 0
                for gi, grp in enumerate(groups):
                    sq, bsq = ssq[gi % 3], b_ssq[gi % 3]
                    tn = grp[0][1]
                    ng = len(grp)
                    banks = []
                    for (c0, tn_, bi) in grp:
                        for hh in range(2):
                            pi = nextps()
                            transposes([(ps[pi][0:tn, c * 128:(c + 1) * 128], xT[:, 4 * hh + c, c0:c0 + tn], ident_f[:])
                                        for c in range(4)], [b_x[bi], b_const], [b_ps[pi]])
                            banks.append(pi)
                    for q_, pi in enumerate(banks):
                        k.op("act", lambda e, q_=q_, pi=pi: e.activation(out=junk[0:tn, :], in_=ps[pi][0:tn, :], func=AF.Square,
                                                                         accum_out=sq[0:tn, q_:q_ + 1]),
                             [b_ps[pi]], [b_junk, bsq])
                    sv = sq[0:tn, 0:2 * ng].rearrange("p (t h) -> p t h", h=2)
                    tt(sq[0:tn, 4:4 + ng], sv[:, :, 0], sv[:, :, 1], ALU.add, [bsq], [bsq])
                    act(AF.Ln, sq[0:tn, 4:4 + ng], sq[0:tn, 4:4 + ng], [bsq, b_const], [bsq], scale=1.0 / D, bias=epsc[0:tn, 0:1])
                    act(AF.Exp, sq[0:tn, 6:6 + ng], sq[0:tn, 4:4 + ng], [bsq], [bsq], scale=-0.5)
                    for t_, (c0, tn_, bi) in enumerate(grp):
                        oi = oidx % 3
                        oidx += 1
                        for hh in range(2):
                            pi = banks[2 * t_ + hh]
                            stt(ost[oi][0:tn, hh * 512:(hh + 1) * 512], ps[pi][0:tn, :], sq[0:tn, 6 + t_:7 + t_],
                                gfb[0:tn, hh * 512:(hh + 1) * 512], ALU.mult, ALU.mult, [b_ps[pi], bsq, b_gfb], [b_ost[oi]])
                        if bi < 2:
                            r0 = half * HALF + c0
                            k.dma("sp", yp[r0:r0 + 128, :], ost[oi][:], reads=[b_ost[oi]])
                        else:
                            k.dma("sp", ys, ost[oi][0:NS, :], reads=[b_ost[oi]])
                k.retire(b_ost + b_ssq + [b_junk, b_gfb])


        k.finish()
    return nc


_W_NAMES = ["norm_mix_g", "w_in", "dw_a", "dw_a_bias", "ln_a_g", "ln_a_b", "conv_b_w", "ln_c_g", "ln_c_b",
            "w_s", "b_s", "w_o", "norm_ffn_g", "w_ffn_in", "w_ffn_out", "norm_final_g"]


def kernel(**inputs):
    f = lambda a: np.ascontiguousarray(np.asarray(a, dtype=np.float32))
    x_prompt = f(inputs["x_prompt"])
    x_sample = f(inputs["x_sample"])
    sca = f(inputs["state_conv_a"])
    scb = f(inputs["state_conv_b"])
    weights = {n: f(inputs[n]) for n in _W_NAMES}
    nc = build_program()
    in_maps = []
    for c in range(8):
        m = dict(weights)
        m["xp"] = x_prompt[c]
        m["xs"] = np.ascontiguousarray(x_sample[c * NS:(c + 1) * NS, 0, :])
        m["sca"] = np.ascontiguousarray(sca[:, c * NS:(c + 1) * NS])
        m["scb"] = np.ascontiguousarray(scb[:, c * NS:(c + 1) * NS])
        in_maps.append(m)
    res = run_bass_kernel_spmd(nc, in_maps, core_ids=list(range(8)))
    R = res.results
    y_prompt = np.stack([np.asarray(R[c]["yp"], dtype=np.float32) for c in range(8)], axis=0)
    y_sample = np.concatenate([np.asarray(R[c]["ys"], dtype=np.float32) for c in range(8)], axis=0)[:, None, :]
    ncap = np.stack([np.asarray(R[c]["ncap"], dtype=np.float32) for c in range(8)], axis=1)
    ncbp = np.stack([np.asarray(R[c]["ncbp"], dtype=np.float32) for c in range(8)], axis=1)
    ncas = np.concatenate([np.asarray(R[c]["ncas"], dtype=np.float32) for c in range(8)], axis=1)
    ncbs = np.concatenate([np.asarray(R[c]["ncbs"], dtype=np.float32) for c in range(8)], axis=1)
    ncv = np.concatenate([np.asarray(R[c]["ncv"], dtype=np.float32) for c in range(8)], axis=1)[:, :, None, :]
    return (y_prompt, y_sample, ncap, ncbp, ncas, ncbs, ncv)
```
